# Optimizing a Trainium2 kernel written in Bass

```python
import math
import jax
import jax.numpy as jnp
from jax import lax
import numpy as np

D_MODEL = 2048
BATCH = 1
SEQ = 8192
DEPTH = 2

EPS = 1e-6

POOL_WINDOWS = (2, 4, 8, 16)
POOL_GROUPS = len(POOL_WINDOWS)
POOL_WIDTH = D_MODEL // 2
POOL_GC = POOL_WIDTH // POOL_GROUPS

DSA_HEADS = 16
DSA_HEAD_DIM = 64
DSA_WIDTH = DSA_HEADS * DSA_HEAD_DIM
IDX_HEADS = 16
IDX_DIM = 64
DSA_MAX_TOPK = 256
Q_BLOCK = 128

SWA_Q_HEADS = 32
SWA_KV_HEADS = 4
SWA_HEAD_DIM = 64
SWA_WIDTH = SWA_Q_HEADS * SWA_HEAD_DIM
WINDOW = 128

NUM_BUCKETS = 32
MAX_DISTANCE = 1024
BIAS_HEADS = 32

EVEN_SPLITS = (POOL_WIDTH, POOL_WIDTH,
               DSA_WIDTH, DSA_HEAD_DIM, DSA_HEAD_DIM,
               DSA_WIDTH,
               IDX_HEADS * IDX_DIM, IDX_DIM, IDX_HEADS)
EVEN_IN = sum(EVEN_SPLITS)
EVEN_OUT_IN = POOL_WIDTH + DSA_WIDTH
ODD_SPLITS = (SWA_WIDTH, SWA_KV_HEADS * SWA_HEAD_DIM, SWA_KV_HEADS * SWA_HEAD_DIM, SWA_WIDTH)
ODD_IN = sum(ODD_SPLITS)

kernel_name = "hybrid_pool_dsa_swa_sink_block"


def rms_norm(x, g):
    xf = x.astype(jnp.float32)
    y = xf * lax.rsqrt(jnp.mean(xf * xf, axis=-1, keepdims=True) + EPS) * g.astype(jnp.float32)
    return y.astype(x.dtype)


def split_cols(h, sizes):
    return jnp.split(h, np.cumsum(sizes)[:-1].tolist(), axis=-1)


def t5_bucket(n):
    n = jnp.maximum(n, 0)
    max_exact = NUM_BUCKETS // 2
    nf = jnp.maximum(n, 1).astype(jnp.float32)
    large = max_exact + (jnp.log(nf / max_exact) / math.log(MAX_DISTANCE / max_exact)
                         * (NUM_BUCKETS - max_exact)).astype(jnp.int32)
    large = jnp.minimum(large, NUM_BUCKETS - 1)
    return jnp.where(n < max_exact, n, large)


def causal_multiscale_pool(a):
    B, T, G, C = a.shape
    cs = jnp.cumsum(a.astype(jnp.float32), axis=1)
    csp = jnp.concatenate([jnp.zeros((B, 1, G, C), jnp.float32), cs], axis=1)
    t = jnp.arange(T)
    outs = []
    for g, w in enumerate(POOL_WINDOWS):
        lo = jnp.maximum(t + 1 - w, 0)
        s = csp[:, t + 1, g] - csp[:, lo, g]
        cnt = (t + 1 - lo).astype(jnp.float32)
        outs.append(s / cnt[None, :, None])
    return jnp.stack(outs, axis=2).astype(a.dtype)


def dsa_attention(q, k, v, iq, ik, iw, rel_bias, topk):
    B, T, H, d = q.shape
    nb = T // Q_BLOCK
    scale = DSA_HEAD_DIM ** -0.5
    keys = jnp.arange(T)

    def to_blocks(z):
        return jnp.swapaxes(z.reshape((B, nb, Q_BLOCK) + z.shape[2:]), 0, 1)

    def block_fn(args):
        qb, iqb, iwb, start = args
        pos_q = start + jnp.arange(Q_BLOCK)
        s_idx = jnp.einsum('bqhe,bse->bqhs', iqb, ik).astype(jnp.float32) * (IDX_DIM ** -0.5)
        score = jnp.einsum('bqhs,bqh->bqs', jax.nn.relu(s_idx), iwb.astype(jnp.float32))
        score = jnp.where(keys[None, None, :] <= pos_q[None, :, None], score, -jnp.inf)
        _, idx = lax.top_k(score, topk)
        valid = idx <= pos_q[None, :, None]
        kg = jax.vmap(lambda kk, ii: kk[ii])(k, idx)
        vg = jax.vmap(lambda vv, ii: vv[ii])(v, idx)
        logits = jnp.einsum('bqhd,bqkd->bhqk', qb, kg).astype(jnp.float32) * scale
        bias = rel_bias[t5_bucket(pos_q[None, :, None] - idx)][..., :H]
        logits = logits + jnp.transpose(bias, (0, 3, 1, 2)).astype(jnp.float32)
        logits = jnp.where(valid[:, None], logits, -jnp.inf)
        p = jax.nn.softmax(logits, axis=-1).astype(vg.dtype)
        return jnp.einsum('bhqk,bqkd->bqhd', p, vg)

    starts = jnp.arange(nb) * Q_BLOCK
    out = lax.map(block_fn, (to_blocks(q), to_blocks(iq), to_blocks(iw), starts))
    return jnp.swapaxes(out, 0, 1).reshape(B, T, H, d)


def swa_sink_attention(q, k, v, sinks, rel_bias):
    B, T, HQ, d = q.shape
    HKV = k.shape[2]
    G = HQ // HKV
    W = WINDOW
    nb = T // W
    qb = q.reshape(B, nb, W, HKV, G, d)

    def with_prev(z):
        zb = z.reshape(B, nb, W, HKV, d)
        prev = jnp.concatenate([jnp.zeros_like(zb[:, :1]), zb[:, :-1]], axis=1)
        return jnp.concatenate([prev, zb], axis=2)

    kw, vw = with_prev(k), with_prev(v)
    logits = jnp.einsum('bnqhgd,bnkhd->bnhgqk', qb, kw).astype(jnp.float32) * (d ** -0.5)
    i = jnp.arange(W)
    j = jnp.arange(2 * W)
    dist = W + i[:, None] - j[None, :]
    bias = rel_bias[t5_bucket(dist)][..., :HQ]
    bias = jnp.transpose(bias, (2, 0, 1)).reshape(HKV, G, W, 2 * W).astype(jnp.float32)
    in_window = (dist >= 0) & (dist < W)
    has_prev = (jnp.arange(nb)[:, None, None] > 0) | (j[None, None, :] >= W)
    mask = in_window[None] & has_prev
    logits = jnp.where(mask[None, :, None, None], logits + bias, -jnp.inf)
    sink = jnp.broadcast_to(sinks.astype(jnp.float32).reshape(HKV, G)[None, None, :, :, None, None],
                            (B, nb, HKV, G, W, 1))
    p = jax.nn.softmax(jnp.concatenate([logits, sink], axis=-1), axis=-1)[..., :-1]
    out = jnp.einsum('bnhgqk,bnkhd->bnqhgd', p.astype(vw.dtype), vw)
    return out.reshape(B, T, HQ * d)


def even_layer(x, norm_g, w_in, pool_w, pool_scale, q_gain, k_gain, w_out, rel_bias):
    B, T, _ = x.shape
    h = rms_norm(x, norm_g) @ w_in
    p_in, p_gate, dq, dk, dv, d_gate, iq, ik, iw = split_cols(h, EVEN_SPLITS)
    pa = p_in.reshape(B, T, POOL_GROUPS, POOL_GC)
    pooled = causal_multiscale_pool(pa) - pa
    py = jnp.einsum('btgc,gcd->btgd', pooled, pool_w) * pool_scale.reshape(POOL_GROUPS, POOL_GC)
    py = jax.nn.silu(p_gate) * py.reshape(B, T, POOL_WIDTH)
    q = rms_norm(dq.reshape(B, T, DSA_HEADS, DSA_HEAD_DIM), q_gain)
    k = rms_norm(dk, k_gain)
    iq = iq.reshape(B, T, IDX_HEADS, IDX_DIM)
    iw = iw * (IDX_HEADS ** -0.5)
    topk = min(DSA_MAX_TOPK, T // 4)
    o = dsa_attention(q, k, dv, iq, ik, iw, rel_bias, topk)
    dy = jax.nn.silu(d_gate) * o.reshape(B, T, DSA_WIDTH)
    return x + jnp.concatenate([py, dy], axis=-1) @ w_out


def odd_layer(x, norm_g, w_in, q_gain, k_gain, sinks, w_out, rel_bias):
    B, T, _ = x.shape
    h = rms_norm(x, norm_g) @ w_in
    q, k, v, gate = split_cols(h, ODD_SPLITS)
    q = rms_norm(q.reshape(B, T, SWA_Q_HEADS, SWA_HEAD_DIM), q_gain)
    k = rms_norm(k.reshape(B, T, SWA_KV_HEADS, SWA_HEAD_DIM), k_gain)
    v = v.reshape(B, T, SWA_KV_HEADS, SWA_HEAD_DIM)
    o = swa_sink_attention(q, k, v, sinks, rel_bias)
    return x + (jax.nn.silu(gate) * o) @ w_out


def setup_inputs(seed: int = 0) -> dict:
    key = jax.random.key(seed)
    ks = jax.random.split(key, 16)
    n_even = (DEPTH + 1) // 2
    n_odd = DEPTH // 2
    nrm = jax.random.normal
    f32 = jnp.float32
    return {
        "x": nrm(ks[0], (BATCH, SEQ, D_MODEL), f32),
        "rel_bias": 0.5 * nrm(ks[1], (NUM_BUCKETS, BIAS_HEADS), f32),
        "even_norm": 1.0 + 0.02 * nrm(ks[2], (n_even, D_MODEL), f32),
        "even_w_in": nrm(ks[3], (n_even, D_MODEL, EVEN_IN), f32) * D_MODEL ** -0.5,
        "even_pool_w": nrm(ks[4], (n_even, POOL_GROUPS, POOL_GC, POOL_GC), f32) * POOL_GC ** -0.5,
        "even_pool_scale": 1.0 + 0.02 * nrm(ks[5], (n_even, POOL_WIDTH), f32),
        "even_q_gain": 1.0 + 0.02 * nrm(ks[6], (n_even, DSA_HEAD_DIM), f32),
        "even_k_gain": 1.0 + 0.02 * nrm(ks[7], (n_even, DSA_HEAD_DIM), f32),
        "even_w_out": nrm(ks[8], (n_even, EVEN_OUT_IN, D_MODEL), f32) * EVEN_OUT_IN ** -0.5,
        "odd_norm": 1.0 + 0.02 * nrm(ks[9], (n_odd, D_MODEL), f32),
        "odd_w_in": nrm(ks[10], (n_odd, D_MODEL, ODD_IN), f32) * D_MODEL ** -0.5,
        "odd_q_gain": 1.0 + 0.02 * nrm(ks[11], (n_odd, SWA_HEAD_DIM), f32),
        "odd_k_gain": 1.0 + 0.02 * nrm(ks[12], (n_odd, SWA_HEAD_DIM), f32),
        "odd_sinks": nrm(ks[13], (n_odd, SWA_Q_HEADS), f32),
        "odd_w_out": nrm(ks[14], (n_odd, SWA_WIDTH, D_MODEL), f32) * SWA_WIDTH ** -0.5,
    }


def reference(x, rel_bias, even_norm, even_w_in, even_pool_w, even_pool_scale, even_q_gain,
              even_k_gain, even_w_out, odd_norm, odd_w_in, odd_q_gain, odd_k_gain, odd_sinks,
              odd_w_out):
    for layer in range(DEPTH):
        i = layer // 2
        if layer % 2 == 0:
            x = even_layer(x, even_norm[i], even_w_in[i], even_pool_w[i], even_pool_scale[i],
                           even_q_gain[i], even_k_gain[i], even_w_out[i], rel_bias)
        else:
            x = odd_layer(x, odd_norm[i], odd_w_in[i], odd_q_gain[i], odd_k_gain[i],
                          odd_sinks[i], odd_w_out[i], rel_bias)
    return x
```

```python
import numpy as np
import ml_dtypes
from contextlib import ExitStack
import concourse.bass as bass
import concourse.mybir as mybir
from concourse.bass_utils import run_bass_kernel_spmd

F32 = mybir.dt.float32
BF16 = mybir.dt.bfloat16
U32 = mybir.dt.uint32
AF = mybir.ActivationFunctionType
ALU = mybir.AluOpType
AX = mybir.AxisListType
NEG = -1.0e30
D = 2048
EPS = 1e-6
QUEUES = ("sync", "scalar", "vector", "gpsimd", "tensor")


class Prog:
    def __init__(self, nc):
        self.nc = nc
        self.ops = []
        self.res = {}

    def op(self, eng, fn, reads=(), writes=(), dma=False, waw=False):
        i = len(self.ops)
        deps = set()
        for r in reads:
            st = self.res.setdefault(r, [[], []])
            deps.update(st[0])
            st[1].append(i)
        for w in writes:
            st = self.res.setdefault(w, [[], []])
            if st[1]:
                deps.update(st[1])
                deps.update(st[0])
                st[0] = [i]
                st[1] = []
            elif waw:
                deps.update(st[0])
                st[0] = [i]
            else:
                st[0].append(i)
        deps.discard(i)
        self.ops.append(dict(eng=eng, fn=fn, deps=deps, dma=dma, sig=False,
                             key=(writes[0] if (dma and writes) else None)))
        return i

    def emit(self, stack):
        nc = self.nc
        ops = self.ops
        for o in ops:
            keep = set()
            for j in o["deps"]:
                d = ops[j]
                if (not d["dma"]) and (not o["dma"]) and d["eng"] == "tensor" and o["eng"] == "tensor":
                    continue
                keep.add(j)
                d["sig"] = True
            o["deps"] = keep
        cnt = {e: 0 for e in QUEUES}
        dcnt = {}
        for o in ops:
            if not o["sig"]:
                continue
            if o["dma"]:
                k = o["key"]
                dcnt[k] = dcnt.get(k, 0) + 16
                o["sem"] = ("dma", k)
                o["val"] = dcnt[k]
            else:
                cnt[o["eng"]] += 1
                o["sem"] = ("eng", o["eng"])
                o["val"] = cnt[o["eng"]]
        sems = {}
        for o in ops:
            if o["sig"] and o["sem"] not in sems:
                sems[o["sem"]] = stack.enter_context(nc.semaphore("s%d" % len(sems)))
        self.nsem = len(sems)
        block = stack.enter_context(nc.Block())
        per = {e: [o for o in ops if o["eng"] == e] for e in QUEUES}

        def run(e, lst):
            waited = {}
            for o in lst:
                need = {}
                for j in o["deps"]:
                    d = ops[j]
                    need[d["sem"]] = max(need.get(d["sem"], 0), d["val"])
                for s, v in need.items():
                    if waited.get(s, 0) >= v:
                        continue
                    e.wait_ge(sems[s], v)
                    waited[s] = v
                ins = o["fn"](e)
                if o["sig"]:
                    ins.then_inc(sems[o["sem"]], 16 if o["dma"] else 1)

        if per["sync"]:
            @block.sync
            def _(e):
                run(e, per["sync"])
        if per["scalar"]:
            @block.scalar
            def _(e):
                run(e, per["scalar"])
        if per["vector"]:
            @block.vector
            def _(e):
                run(e, per["vector"])
        if per["gpsimd"]:
            @block.gpsimd
            def _(e):
                run(e, per["gpsimd"])
        if per["tensor"]:
            @block.tensor
            def _(e):
                run(e, per["tensor"])


class KB:
    def __init__(self):
        self.nc = bass.Bass("TRN2", target_bir_lowering=False)
        self.st = ExitStack()
        self.P = Prog(self.nc)
        self.outkeys = []
        self.banks = None
        self.gi = 0

    def sb(self, name, shape, dt):
        return self.st.enter_context(self.nc.sbuf_tensor(name, shape, dt))

    def din(self, name, shape, dt):
        return self.nc.dram_tensor(name, list(shape), dt, kind="ExternalInput").ap()

    def dout(self, name, shape, dt):
        return self.nc.dram_tensor(name, list(shape), dt, kind="ExternalOutput").ap()

    def dscr(self, name, shape, dt):
        return self.nc.dram_tensor(name, list(shape), dt, kind="Internal").ap()

    def alloc_banks(self):
        self.banks = [self.st.enter_context(self.nc.psum_tensor("bank%d" % i, [128, 512], F32)) for i in range(8)]

    def G(self):
        i = self.gi % 2
        self.gi += 1
        return self.banks[i], ("bank", i)

    def dma(self, q, out, in_, r, w):
        self.P.op(q, lambda e: e.dma_start(out=out, in_=in_), reads=r, writes=w, dma=True)

    def dma_out(self, q, out, in_, r, key):
        self.P.op(q, lambda e: e.dma_start(out=out, in_=in_), reads=r, writes=[key], dma=True)
        if key not in self.outkeys:
            self.outkeys.append(key)

    def act(self, out, in_, func, r, w, **kw):
        self.P.op("scalar", lambda e: e.activation(out=out, in_=in_, func=func, **kw), reads=r, writes=w)

    def vec(self, name, r, w, *a, **kw):
        self.P.op("vector", lambda e: getattr(e, name)(*a, **kw), reads=r, writes=w)

    def pool(self, name, r, w, *a, **kw):
        self.P.op("gpsimd", lambda e: getattr(e, name)(*a, **kw), reads=r, writes=w)

    def mm(self, out, lhsT, rhs, start, stop, r, w):
        self.P.op("tensor", lambda e: e.matmul(out, lhsT=lhsT, rhs=rhs, start=start, stop=stop), reads=r, writes=w)

    def tr(self, out, in_, ident, r, w):
        self.P.op("tensor", lambda e: e.transpose(out=out, in_=in_, identity=ident), reads=r, writes=w)

    def finish(self):
        self.P.op("sync", lambda e: e.nop(), reads=list(self.outkeys))
        self.P.emit(self.st)
        self.st.close()
        return self.nc

    def rmsnorm_T(self, xt, kx, gB, xnT, kxnT, tmp):
        junk, ss, rstd, xn, idt = tmp["junk"], tmp["ss"], tmp["rstd"], tmp["xn"], tmp["idt"]
        kj = tmp.get("kjunk", "junk")
        self.act(junk, xt, AF.Square, [kx], [kj, "ss"], accum_out=ss[:])
        self.act(rstd[:], ss[:], AF.Sqrt, ["ss"], ["rstd"], scale=1.0 / D, bias=EPS)
        self.vec("reciprocal", ["rstd"], ["rstd"], out=rstd[:], in_=rstd[:])
        self.vec("scalar_tensor_tensor", [kx, "rstd", "gB"], ["xn"], out=xn[:], in0=xt, scalar=rstd[:, 0:1],
                 in1=gB[:], op0=ALU.mult, op1=ALU.mult)
        self.transposeN(xn, "xn", xnT, kxnT, 16, 128, idt)

    def transposeN(self, src, ksrc, dst, kdst, n, width, idt, rows=128):
        per = max(1, 1024 // rows)
        per = min(per, 8)
        i = 0
        while i < n:
            m = min(per, n - i)
            bank, kb = self.G()
            pv = bank[:].bitcast(BF16)[0:width, 0:m * rows].rearrange("p (a b) -> p a b", a=m)
            for a in range(m):
                self.tr(pv[:, a, :], src[0:rows, (i + a) * width:(i + a + 1) * width], idt[0:rows, 0:rows],
                        [ksrc, "idt"], [kb])
            self.vec("tensor_copy", [kb], [kdst], out=dst[0:width, i:i + m, 0:rows], in_=pv)
            i += m
CW = 256
MASKV = -30000.0


def build_l1():
    K = KB()
    nc = K.nc
    x = K.din("x", [8, 128, D], F32)
    g = K.din("g", [1, D], F32)
    w = K.din("w", [D, 192], F32)
    kg = K.din("kg", [1, 64], F32)
    ident = K.din("ident", [128, 128], BF16)
    kT = K.dout("kT", [8, 64, 128], BF16)
    ikT = K.dout("ikT", [8, 64, 128], BF16)
    vo = K.dout("v", [8, 128, 64], BF16)
    K.alloc_banks()
    gB = K.sb("gB", [128, D], F32)
    kgB = K.sb("kgB", [128, 64], F32)
    idt = K.sb("idt", [128, 128], BF16)
    wt = K.sb("wt", [128, 16, 192], BF16)
    xts = [K.sb("xt%d" % i, [128, D], F32) for i in range(2)]
    tmp = dict(junk=K.sb("junk", [128, D], BF16)[:], ss=K.sb("ss", [128, 1], F32), rstd=K.sb("rstd", [128, 1], F32),
               xn=K.sb("xn", [128, D], BF16), idt=idt)
    xnT = K.sb("xnT", [128, 16, 128], BF16)
    kss = K.sb("kss", [128, 1], F32)
    krs = K.sb("krs", [128, 1], F32)
    kjunk = K.sb("kjunk", [128, 64], F32)
    kvb = [K.sb("kvb%d" % i, [128, 128], BF16) for i in range(2)]
    vb = [K.sb("vb%d" % i, [128, 64], BF16) for i in range(2)]
    kTs = [K.sb("kTs%d" % i, [64, 2, 128], BF16) for i in range(2)]

    K.dma("sync", gB[:], g[0:1, :].to_broadcast([128, D]), [], ["gB"])
    K.dma("sync", kgB[:], kg[0:1, :].to_broadcast([128, 64]), [], ["kgB"])
    K.dma("sync", idt[:], ident[:, :], [], ["idt"])
    K.dma("gpsimd", wt[:], w.rearrange("(k p) n -> p k n", p=128), [], ["wt"])
    for b in range(8):
        i = b % 2
        xt = xts[i]
        K.dma("sync", xt[:], x[b], [], [("xt", i)])
        K.rmsnorm_T(xt[:], ("xt", i), gB, xnT, "xnT", tmp)
        bank, kbk = K.G()
        for k in range(16):
            K.mm(bank[:, 0:192], xnT[:, k, :], wt[:, k, :], k == 0, k == 15, ["xnT", "wt"], [kbk])
        K.act(kjunk[:], bank[:, 0:64], AF.Square, [kbk], ["kjunk", "kss"], accum_out=kss[:])
        K.act(krs[:], kss[:], AF.Sqrt, ["kss"], ["krs"], scale=1.0 / 64, bias=EPS)
        K.vec("reciprocal", ["krs"], ["krs"], out=krs[:], in_=krs[:])
        K.vec("scalar_tensor_tensor", [kbk, "krs", "kgB"], [("kvb", i)], out=kvb[i][:, 0:64], in0=bank[:, 0:64],
              scalar=krs[:, 0:1], in1=kgB[:], op0=ALU.mult, op1=ALU.mult)
        K.vec("tensor_copy", [kbk], [("kvb", i)], out=kvb[i][:, 64:128], in_=bank[:, 128:192])
        K.vec("tensor_copy", [kbk], [("vb", i)], out=vb[i][:], in_=bank[:, 64:128])
        K.dma_out("sync", vo[b], vb[i][:], [("vb", i)], ("vo", i))
        K.transposeN(kvb[i], ("kvb", i), kTs[i], ("kTs", i), 2, 64, idt)
        K.dma_out("sync", kT[b], kTs[i][:, 0, :], [("kTs", i)], ("kTo", i))
        K.dma_out("sync", ikT[b], kTs[i][:, 1, :], [("kTs", i)], ("ikTo", i))
    return K.finish()


def blocks_of(c):
    return [8 * j + c for j in range(8)]


def run_l1(x, even_norm, even_w_in, even_k_gain):
    nc = build_l1()
    xb = x.reshape(64, 128, D)
    wkv = np.ascontiguousarray(np.concatenate([even_w_in[0][:, 3072:3200], even_w_in[0][:, 5248:5312]], axis=1))
    ident = np.eye(128, dtype=np.float32).astype(ml_dtypes.bfloat16)
    in_maps = []
    for c in range(8):
        in_maps.append({"x": np.ascontiguousarray(xb[blocks_of(c)]), "g": even_norm[0:1], "w": wkv,
                        "kg": even_k_gain[0:1], "ident": ident})
    res = run_bass_kernel_spmd(nc, in_maps, core_ids=list(range(8)))
    kT = np.zeros((64, 64, 128), ml_dtypes.bfloat16)
    ikT = np.zeros((64, 64, 128), ml_dtypes.bfloat16)
    v = np.zeros((64, 128, 64), ml_dtypes.bfloat16)
    for c in range(8):
        r = res.results[c]
        kT[blocks_of(c)] = r["kT"]
        ikT[blocks_of(c)] = r["ikT"]
        v[blocks_of(c)] = r["v"]
    return kT, ikT, v


RNG = 32.0
NIT = 30

EVEN_CHUNKS = [("pin", 0, 4), ("pgate", 1024, 4), ("dq", 2048, 4), ("dgate", 3200, 4), ("iq", 4224, 4)]


def build_l2(nj=8):
    K = KB()
    nc = K.nc
    P = K.P
    x_own = K.din("x_own", [8, 128, D], F32)
    x_prev = K.din("x_prev", [8, 128, D], F32)
    g = K.din("g", [1, D], F32)
    w_in = K.din("w_in", [D, 5328], F32)
    pool_w = K.din("pool_w", [4, 256, 256], F32)
    pscale = K.din("pscale", [1, 1024], F32)
    qgain = K.din("qgain", [1, 64], F32)
    w_out = K.din("w_out", [D, D], F32)
    kik = K.din("kik", [128, 8192], BF16)
    vs = K.din("vs", [64, 128, 64], BF16)
    U = K.din("U", [128, 16, 1152], F32)
    rb31 = K.din("rb31", [1, 16], F32)
    ident = K.din("ident", [128, 128], BF16)
    i4 = K.din("i4", [128, 512], BF16)
    diagm = K.din("diagm", [128, 128], F32)
    vadd = K.din("vadd", [1, 1024], F32)
    ktin = K.din("kt", [128, 8], F32)
    poolA = K.din("poolA", [8, 2, 128, 512], BF16)
    invcnt = K.din("invcnt", [8, 1, 512], F32)
    x1 = K.dout("x1", [8, 128, D], F32)
    wbi = K.dscr("wbi", [D, 5328], BF16)
    wbo = K.dscr("wbo", [D, D], BF16)
    bts = K.dscr("bts", [8, 2, 128, 2, 1024], BF16)

    G = [K.st.enter_context(nc.psum_tensor("G%d" % i, [128, 512], F32)) for i in range(2)]
    K.banks = G
    Lb = [K.st.enter_context(nc.psum_tensor("L%d" % i, [128, 1024], F32)) for i in range(2)]
    Ob = K.st.enter_context(nc.psum_tensor("O", [128, 8, 128], F32))
    gB = K.sb("gB", [128, D], F32)
    idt = K.sb("idt", [128, 128], BF16)
    I4 = K.sb("I4", [128, 512], BF16)
    KIK = K.sb("KIK", [128, 8192], BF16)
    Vaug = K.sb("Vaug", [128, 64, 65], BF16)
    score = K.sb("score", [128, 8192], F32)
    mb = K.sb("mb", [128, 8192], BF16)
    xt = K.sb("xt", [128, D], F32)
    xn = K.sb("xn", [128, D], BF16)
    xnT = K.sb("xnT", [128, 16, 128], BF16)
    xnTp = K.sb("xnTp", [128, 16, 128], BF16)
    wts = [K.sb("wt%d" % i, [128, 16, CW], BF16) for i in range(2)]
    pa_own = K.sb("pa_own", [128, 1024], BF16)
    pa_prev = K.sb("pa_prev", [128, 1024], BF16)
    sgp = K.sb("sgp", [128, 1024], F32)
    dqf = K.sb("dqf", [128, 1024], F32)
    sgd = K.sb("sgd", [128, 1024], F32)
    iwc = K.sb("iwc", [128, 16], F32)
    C = K.sb("C", [128, 16, 128], BF16)
    QIQ = K.sb("QIQ", [128, 16, 128], BF16)
    rb = [K.sb("r%d" % i, [128, 512], F32) for i in range(2)]
    PT = [K.sb("PT%d" % i, [128, 1024], BF16) for i in range(2)]
    bstage = K.sb("bstage", [128, 8, 128], F32)
    bstage2 = K.sb("bstage2", [128, 8, 128], F32)
    bhl = [K.sb("bhl%d" % i, [128, 2, 1024], BF16) for i in range(2)]
    rb31B = K.sb("rb31B", [128, 16], F32)
    pooledT = K.sb("pooledT", [128, 8, 128], BF16)
    pA = K.sb("pA", [128, 2, 512], BF16)
    invc = K.sb("invc", [128, 512], F32)
    pw = K.sb("pw", [128, 4, 2, 256], BF16)
    pwf = K.sb("pwf", [128, 4, 2, 256], F32)
    qg8 = K.sb("qg8", [128, 64], F32)
    vaddB = K.sb("vaddB", [128, 1024], F32)
    dm = K.sb("dm", [128, 128], F32)
    ktc = K.sb("ktc", [128, 8], F32)
    sm = {n: K.sb(n, [128, 1], F32) for n in ["ss", "rstd", "lo", "thr", "cnt"]}
    flag = K.sb("flag", [128, 1], U32)
    qss = K.sb("qss", [128, 16], F32)
    qrs = K.sb("qrs", [128, 16], F32)
    rden = K.sb("rden", [128, 8], F32)
    of = K.sb("of", [128, 8, 64], F32)
    tmp = dict(junk=mb[:, 0:D], kjunk="mb", ss=sm["ss"], rstd=sm["rstd"], xn=xn, idt=idt)
    SK = [("score", c) for c in range(16)]

    K.dma("sync", gB[:], g[0:1, :].to_broadcast([128, D]), [], ["gB"])
    K.dma("sync", idt[:], ident[:, :], [], ["idt"])
    K.dma("sync", I4[:], i4[:, :], [], ["I4"])
    K.dma("sync", KIK[:], kik[:, :], [], ["KIK"])
    K.dma("sync", Vaug[:, :, 0:64], vs.rearrange("u p d -> p u d"), [], ["Vaug"])
    K.vec("memset", [], ["Vaug1"], Vaug[:, :, 64:65], 1.0)
    K.dma("sync", rb31B[:], rb31[0:1, :].to_broadcast([128, 16]), [], ["rb31B"])
    K.dma("sync", qg8[:], qgain[0:1, :].to_broadcast([128, 64]), [], ["qg8"])
    K.vec("tensor_scalar_mul", ["qg8"], ["qg8"], out=qg8[:], in0=qg8[:], scalar1=0.125)
    K.dma("sync", vaddB[:], vadd[0:1, :].to_broadcast([128, 1024]), [], ["vaddB"])
    K.dma("sync", dm[:], diagm[:, :], [], ["dm"])
    K.dma("sync", ktc[:], ktin[:, :], [], ["ktc"])
    K.dma("sync", pwf[:], pool_w.rearrange("g (k p) d -> p g k d", p=128), [], ["pwf"])
    K.dma("sync", score[:, 0:1024], pscale[0:1, :].to_broadcast([128, 1024]), [], [SK[0]])
    K.vec("tensor_copy", [SK[0]], [SK[1]], out=sm["cnt"][:], in_=score[:, 0:1])
    for gg in range(4):
        K.vec("tensor_tensor", ["pwf", SK[0]], ["pw"], out=pw[:, gg, :, :], in0=pwf[:, gg, :, :],
              in1=score[:, gg * 256:(gg + 1) * 256].unsqueeze(1).to_broadcast([128, 2, 256]), op=ALU.mult)
    for c0 in range(0, 5328, 512):
        c1 = min(c0 + 512, 5328)
        K.dma("gpsimd", wbi[:, c0:c1], w_in[:, c0:c1], [], ["wbi"])
    for c0 in range(0, D, 512):
        K.dma("gpsimd", wbo[:, c0:c0 + 512], w_out[:, c0:c0 + 512], [], ["wbo"])
    for dl in range(8):
        for hh in range(2):
            hs = slice(hh * 8, hh * 8 + 8)
            K.dma("sync", bstage[:], U[:, hs, dl * 128:(dl + 1) * 128], [], ["bstage"])
            K.vec("tensor_tensor", ["bstage", "rb31B"], ["bstage"], out=bstage[:], in0=bstage[:],
                  in1=rb31B[:, hs].unsqueeze(2).to_broadcast([128, 8, 128]), op=ALU.subtract)
            i = (dl * 2 + hh) % 2
            bv = bhl[i]
            kb_ = ("bhl", i)
            K.vec("tensor_copy", ["bstage"], [kb_], out=bv[:, 0, :], in_=bstage[:].rearrange("p a b -> p (a b)"))
            K.vec("tensor_tensor", ["bstage", kb_], ["bstage2"], out=bstage2[:].rearrange("p a b -> p (a b)"),
                  in0=bstage[:].rearrange("p a b -> p (a b)"), in1=bv[:, 0, :], op=ALU.subtract)
            K.vec("tensor_copy", ["bstage2"], [kb_], out=bv[:, 1, :], in_=bstage2[:].rearrange("p a b -> p (a b)"))
            K.dma("sync", bts[dl, hh], bv[:], [kb_], ["bts"])

    wslot = [0]

    def load_w(src, c0, rkey):
        i = wslot[0] % 2
        wslot[0] += 1
        K.dma("sync", wts[i][:], src[:, c0:c0 + CW].rearrange("(k p) n -> p k n", p=128), [rkey], [("wt", i)])
        return wts[i], ("wt", i)

    for j in range(nj):
        N = 1024 * (j + 1)
        nkb = 8 * (j + 1)
        nch = 2 * (j + 1)
        K.dma("sync", xt[:], x_prev[j], [], ["xt"])
        K.rmsnorm_T(xt[:], "xt", gB, xnTp, "xnTp", tmp)
        K.dma("sync", xt[:], x_own[j], [], ["xt"])
        K.rmsnorm_T(xt[:], "xt", gB, xnT, "xnT", tmp)
        K.dma("sync", pA[:], poolA[j].rearrange("a p n -> p a n"), [], ["pA"])
        K.dma("sync", invc[:], invcnt[j, 0:1, :].to_broadcast([128, 512]), [], ["invc"])
        for kind, col0, nchunk in EVEN_CHUNKS:
            for ci in range(nchunk):
                c0 = col0 + ci * CW
                wt, kw = load_w(wbi, c0, "wbi")
                bank, kbk = K.G()
                for k in range(16):
                    K.mm(bank[:, 0:CW], xnT[:, k, :], wt[:, k, :], k == 0, k == 15, ["xnT", kw], [kbk])
                lo_, hi_ = ci * CW, (ci + 1) * CW
                if kind == "pin":
                    K.vec("tensor_copy", [kbk], ["pa_own"], out=pa_own[:, lo_:hi_], in_=bank[:, 0:CW])
                    bank2, kb2 = K.G()
                    for k in range(16):
                        K.mm(bank2[:, 0:CW], xnTp[:, k, :], wt[:, k, :], k == 0, k == 15, ["xnTp", kw], [kb2])
                    K.vec("tensor_copy", [kb2], ["pa_prev"], out=pa_prev[:, lo_:hi_], in_=bank2[:, 0:CW])
                elif kind == "pgate":
                    K.act(sgp[:, lo_:hi_], bank[:, 0:CW], AF.Silu, [kbk], ["sgp"])
                elif kind == "dq":
                    K.vec("tensor_copy", [kbk], ["dqf"], out=dqf[:, lo_:hi_], in_=bank[:, 0:CW])
                elif kind == "dgate":
                    K.act(sgd[:, lo_:hi_], bank[:, 0:CW], AF.Silu, [kbk], ["sgd"])
                elif kind == "iq":
                    K.vec("tensor_copy", [kbk], ["C"], out=C[:, ci * 4:(ci + 1) * 4, 0:64],
                          in_=bank[:, 0:CW].rearrange("p (h d) -> p h d", d=64))
        i = wslot[0] % 2
        wslot[0] += 1
        K.dma("sync", wts[i][:, :, 0:16], wbi[:, 5312:5328].rearrange("(k p) n -> p k n", p=128), ["wbi"], [("wt", i)])
        bank, kbk = K.G()
        for k in range(16):
            K.mm(bank[:, 0:16], xnT[:, k, :], wts[i][:, k, 0:16], k == 0, k == 15, ["xnT", ("wt", i)], [kbk])
        K.vec("tensor_scalar_mul", [kbk], ["iwc"], out=iwc[:], in0=bank[:, 0:16], scalar1=1.0 / 32.0)
        K.act(score[:, 0:1024], dqf[:], AF.Square, ["dqf"], [SK[0], SK[1]])
        K.vec("tensor_reduce", [SK[0], SK[1]], ["qss"], out=qss[:], in_=score[:, 0:1024].rearrange("p (h d) -> p h d", d=64),
              axis=AX.X, op=ALU.add)
        K.act(qrs[:], qss[:], AF.Sqrt, ["qss"], ["qrs"], scale=1.0 / 64, bias=EPS)
        K.vec("reciprocal", ["qrs"], ["qrs"], out=qrs[:], in_=qrs[:])
        dq3 = dqf[:].rearrange("p (h d) -> p h d", d=64)
        K.vec("tensor_tensor", ["dqf", "qrs"], ["dqf"], out=dq3, in0=dq3,
              in1=qrs[:].unsqueeze(2).to_broadcast([128, 16, 64]), op=ALU.mult)
        K.vec("tensor_tensor", ["dqf", "qg8"], ["C"], out=C[:, :, 64:128], in0=dq3,
              in1=qg8[:].unsqueeze(1).to_broadcast([128, 16, 64]), op=ALU.mult)
        K.transposeN(C[:].rearrange("p h d -> p (h d)"), "C", QIQ, "QIQ", 16, 128, idt)
        for half in range(2):
            bank, kbk = K.G()
            for gi in range(2):
                gg = half * 2 + gi
                for kc in range(2):
                    cs = slice(gg * 256 + kc * 128, gg * 256 + (kc + 1) * 128)
                    o_ = bank[:, (gi * 2 + kc) * 128:(gi * 2 + kc + 1) * 128]
                    K.mm(o_, pa_prev[:, cs], pA[:, 0, gg * 128:(gg + 1) * 128], True, False, ["pa_prev", "pA"], [kbk])
                    K.mm(o_, pa_own[:, cs], pA[:, 1, gg * 128:(gg + 1) * 128], False, True, ["pa_own", "pA"], [kbk])
            K.vec("tensor_tensor", [kbk, "invc"], ["pooledT"], out=pooledT[:, half * 4:(half + 1) * 4, :].rearrange("p (g k) t -> p g k t", g=2),
                  in0=bank[:].rearrange("p (g k t) -> p g k t", g=2, k=2),
                  in1=invc[:, half * 256:(half + 1) * 256].rearrange("p (g t) -> p g t", g=2).unsqueeze(2).to_broadcast([128, 2, 2, 128]),
                  op=ALU.mult)
        for half in range(2):
            bank, kbk = K.G()
            for gi in range(2):
                gg = half * 2 + gi
                for kc in range(2):
                    K.mm(bank[:, gi * 256:(gi + 1) * 256], pooledT[:, gg * 2 + kc, :], pw[:, gg, kc, :], kc == 0, kc == 1,
                         ["pooledT", "pw"], [kbk])
            K.vec("tensor_tensor", [kbk, "sgp"], ["xn"], out=xn[:, half * 512:(half + 1) * 512], in0=bank[:],
                  in1=sgp[:, half * 512:(half + 1) * 512], op=ALU.mult)
        ri = 0
        for ch in range(nch):
            for h in range(16):
                bank, kbk = K.G()
                K.mm(bank[:], QIQ[0:64, h, :], KIK[0:64, ch * 512:(ch + 1) * 512], True, True, ["QIQ", "KIK"], [kbk])
                r = rb[ri % 2]
                kr = ("r", ri % 2)
                ri += 1
                K.act(r[:], bank[:], AF.Relu, [kbk], [kr])
                sc = score[:, ch * 512:(ch + 1) * 512]
                if h == 0:
                    K.vec("tensor_scalar_mul", [kr, "iwc"], [SK[ch]], out=sc, in0=r[:], scalar1=iwc[:, 0:1])
                else:
                    K.vec("scalar_tensor_tensor", [kr, "iwc", SK[ch]], [SK[ch]], out=sc, in0=r[:], scalar=iwc[:, h:h + 1],
                          in1=sc, op0=ALU.mult, op1=ALU.add)
        K.vec("tensor_tensor", [SK[0], SK[1], "vaddB"], [SK[0], SK[1]], out=score[:, 0:1024], in0=score[:, 0:1024],
              in1=vaddB[:], op=ALU.add)
        K.vec("tensor_tensor", [SK[nch - 1], "dm"], [SK[nch - 1]], out=score[:, N - 128:N], in0=score[:, N - 128:N],
              in1=dm[:], op=ALU.add)
        SKn = SK[:nch]
        lo, thr, cnt = sm["lo"], sm["thr"], sm["cnt"]
        K.vec("memset", [], ["lo"], lo[:], -RNG)
        for it in range(NIT):
            step = 2.0 * RNG / (2.0 ** (it + 1))
            K.vec("tensor_scalar_add", ["lo"], ["thr"], out=thr[:], in0=lo[:], scalar1=step)
            K.vec("tensor_scalar", SKn + ["thr"], ["mb", "cnt"], out=mb[:, 0:N], in0=score[:, 0:N], scalar1=thr[:, 0:1],
                  scalar2=None, op0=ALU.is_ge, op1=ALU.add, accum_out=cnt[:])
            K.vec("tensor_tensor", ["cnt", "ktc"], ["flag"], out=flag[:], in0=cnt[:], in1=ktc[:, j:j + 1], op=ALU.is_ge)
            K.vec("copy_predicated", ["flag", "thr"], ["lo"], out=lo[:], mask=flag[:], data=thr[:])
        K.vec("tensor_scalar", SKn + ["lo"], ["mb"], out=mb[:, 0:N], in0=score[:, 0:N], scalar1=lo[:, 0:1],
              scalar2=MASKV, op0=ALU.is_lt, op1=ALU.mult)
        li = 0
        bi = 0
        for hh in range(2):
            for kb in range(nkb):
                dl = nkb - 1 - kb
                L = Lb[li % 2]
                kL = ("L", li % 2)
                pt = PT[li % 2]
                kpt = ("PT", li % 2)
                li += 1
                near = dl < 8
                if near:
                    bv = bhl[bi % 2]
                    kbv = ("bhl", bi % 2)
                    bi += 1
                    K.dma("sync", bv[:], bts[dl, hh], ["bts"], [kbv])
                for n in range(2):
                    Ls = L[:, n * 512:(n + 1) * 512]
                    h0 = hh * 8 + n * 4
                    K.mm(Ls, KIK[64:128, kb * 128:(kb + 1) * 128], QIQ[64:128, h0:h0 + 4, :], True, False,
                         ["KIK", "QIQ"], [kL])
                    K.mm(Ls, mb[:, kb * 128:(kb + 1) * 128], I4[:], False, not near, ["mb", "I4"], [kL])
                    if near:
                        K.mm(Ls, idt[:], bv[:, 0, n * 512:(n + 1) * 512], False, False, ["idt", kbv], [kL])
                        K.mm(Ls, idt[:], bv[:, 1, n * 512:(n + 1) * 512], False, True, ["idt", kbv], [kL])
                K.act(pt[:], L[:], AF.Exp, [kL], [kpt])
                for i8 in range(8):
                    K.mm(Ob[:, i8, 0:65], pt[:, i8 * 128:(i8 + 1) * 128], Vaug[:, kb, :], (kb == 0 and i8 % 4 == 0), kb == nkb - 1,
                         [kpt, "Vaug", "Vaug1"], ["O"])
            K.vec("reciprocal", ["O"], ["rden"], out=rden[:], in_=Ob[:, :, 64])
            K.vec("tensor_tensor", ["O", "rden"], ["of"], out=of[:], in0=Ob[:, :, 0:64],
                  in1=rden[:].unsqueeze(2).to_broadcast([128, 8, 64]), op=ALU.mult)
            K.vec("tensor_tensor", ["of", "sgd"], ["xn"], out=xn[:, 1024 + hh * 512:1024 + (hh + 1) * 512],
                  in0=of[:].rearrange("p h d -> p (h d)"), in1=sgd[:, hh * 512:(hh + 1) * 512], op=ALU.mult)
        K.transposeN(xn, "xn", xnT, "xnT", 16, 128, idt)
        for n in range(D // CW):
            wt, kw = load_w(wbo, n * CW, "wbo")
            bank, kbk = K.G()
            for k in range(16):
                K.mm(bank[:, 0:CW], xnT[:, k, :], wt[:, k, :], k == 0, k == 15, ["xnT", kw], [kbk])
            K.vec("tensor_tensor", [kbk, "xt"], ["xt"], out=xt[:, n * CW:(n + 1) * CW], in0=bank[:, 0:CW],
                  in1=xt[:, n * CW:(n + 1) * CW], op=ALU.add)
        K.dma_out("sync", x1[j], xt[:], ["xt"], "x1")
    return K.finish()


def bucket_np(n):
    import math
    n = np.maximum(n, 0)
    nf = np.maximum(n, 1).astype(np.float32)
    large = 16 + (np.log(nf / np.float32(16)) / np.float32(math.log(64)) * np.float32(16)).astype(np.int32)
    large = np.minimum(large, 31)
    return np.where(n < 16, n, large)


def l2_inputs(c, x, rel_bias, even_norm, even_w_in, even_pool_w, even_pool_scale, even_q_gain, even_w_out, kT, ikT, v):
    bf = ml_dtypes.bfloat16
    xb = x.reshape(64, 128, D)
    blks = blocks_of(c)
    x_own = np.ascontiguousarray(xb[blks])
    x_prev = np.stack([xb[b - 1] if b > 0 else np.zeros((128, D), np.float32) for b in blks])
    kik = np.zeros((128, 64, 128), bf)
    vs = np.zeros((64, 128, 64), bf)
    for u in range(64):
        kb = u + c - 7
        if 0 <= kb < 64:
            kik[0:64, u, :] = ikT[kb]
            kik[64:128, u, :] = kT[kb]
            vs[u] = v[kb]
    s = np.arange(128)[:, None]
    n = np.arange(1152)[None, :]
    U = np.ascontiguousarray(rel_bias[bucket_np(np.maximum(n - s, 0))][:, :, :16].transpose(0, 2, 1)).astype(np.float32)
    q = np.arange(128)[:, None]
    ss = np.arange(128)[None, :]
    diagm = np.where(ss <= q, 0.0, NEG).astype(np.float32)
    vadd = np.zeros((1, 1024), np.float32)
    for u in range(8):
        if u + c - 7 < 0:
            vadd[0, u * 128:(u + 1) * 128] = NEG
    kt = np.zeros((128, 8), np.float32)
    poolA = np.zeros((8, 2, 128, 4, 128), np.float32)
    invcnt = np.zeros((8, 1, 4, 128), np.float32)
    for j, b in enumerate(blks):
        tg = b * 128 + np.arange(128)
        kt[:, j] = np.minimum(256, tg + 1)
        for gi, w in enumerate((2, 4, 8, 16)):
            cnt = np.minimum(w, tg + 1)
            invcnt[j, 0, gi] = 1.0 / cnt
            t = np.arange(128)[None, :]
            s_ = np.arange(128)[:, None]
            own = ((s_ <= t) & (s_ > t - w)).astype(np.float32) - np.where(s_ == t, cnt[None, :], 0)
            prev = ((s_ - 128 > t - w)).astype(np.float32) if b > 0 else np.zeros((128, 128), np.float32)
            poolA[j, 0, :, gi, :] = prev
            poolA[j, 1, :, gi, :] = own
    return {
        "x_own": x_own, "x_prev": np.ascontiguousarray(x_prev), "g": even_norm[0:1], "w_in": even_w_in[0],
        "pool_w": even_pool_w[0], "pscale": even_pool_scale[0:1], "qgain": even_q_gain[0:1], "w_out": even_w_out[0],
        "kik": np.ascontiguousarray(kik.reshape(128, 8192)), "vs": vs, "U": U,
        "rb31": np.ascontiguousarray(rel_bias[31:32, :16]),
        "ident": np.eye(128, dtype=np.float32).astype(bf),
        "i4": np.tile(np.eye(128, dtype=np.float32), (1, 4)).astype(bf),
        "diagm": diagm, "vadd": vadd, "kt": kt,
        "poolA": poolA.reshape(8, 2, 128, 512).astype(bf), "invcnt": invcnt.reshape(8, 1, 512),
    }


def build_l3(nj=8):
    K = KB()
    nc = K.nc
    x_own = K.din("x_own", [8, 128, D], F32)
    x_prev = K.din("x_prev", [8, 128, D], F32)
    g = K.din("g", [1, D], F32)
    w_in = K.din("w_in", [D, 4608], F32)
    qgain = K.din("qgain", [1, 64], F32)
    kgain = K.din("kgain", [1, 64], F32)
    sinks = K.din("sinks", [1, 32], F32)
    w_out = K.din("w_out", [D, D], F32)
    mbt_in = K.din("mbt", [128, 2, 32, 128], F32)
    flagp = K.din("flagp", [128, 8], F32)
    ident = K.din("ident", [128, 128], BF16)
    y = K.dout("y", [8, 128, D], F32)
    wbi = K.dscr("wbi", [D, 4608], BF16)
    wbo = K.dscr("wbo", [D, D], BF16)

    G = [K.st.enter_context(nc.psum_tensor("G%d" % i, [128, 512], F32)) for i in range(2)]
    K.banks = G
    Lb = [K.st.enter_context(nc.psum_tensor("L%d" % i, [128, 1024], F32)) for i in range(2)]
    Ob = K.st.enter_context(nc.psum_tensor("O", [128, 8, 128], F32))

    gB = K.sb("gB", [128, D], F32)
    idt = K.sb("idt", [128, 128], BF16)
    xt = K.sb("xt", [128, D], F32)
    xn = K.sb("xn", [128, D], BF16)
    junk = K.sb("junk", [128, D], BF16)
    xnT = K.sb("xnT", [128, 16, 128], BF16)
    xnTp = K.sb("xnTp", [128, 16, 128], BF16)
    wts = [K.sb("wt%d" % i, [128, 16, CW], BF16) for i in range(2)]
    qf = K.sb("qf", [128, D], F32)
    sq = K.sb("sq", [128, D], F32)
    sg = K.sb("sg", [128, D], F32)
    kf = [K.sb("kf%d" % i, [128, 256], F32) for i in range(2)]
    Cq = K.sb("Cq", [128, D], BF16)
    Ck = [K.sb("Ck%d" % i, [128, 256], BF16) for i in range(2)]
    qT = K.sb("qT", [64, 32, 128], BF16)
    kT = [K.sb("kT%d" % i, [64, 4, 128], BF16) for i in range(2)]
    Vaug = [K.sb("Vaug%d" % i, [128, 4, 65], BF16) for i in range(2)]
    MBT = K.sb("MBT", [128, 2, 32, 128], F32)
    S = [K.sb("S%d" % i, [128, 1024], F32) for i in range(2)]
    PT = [K.sb("PT%d" % i, [128, 1024], BF16) for i in range(2)]
    qg8 = K.sb("qg8", [128, 64], F32)
    kgB = K.sb("kgB", [128, 64], F32)
    sinkE = K.sb("sinkE", [128, 32], F32)
    flg = K.sb("flg", [128, 8], F32)
    sm = {n: K.sb(n, [128, 1], F32) for n in ["ss", "rstd"]}
    qss = K.sb("qss", [128, 32], F32)
    kss = K.sb("kss", [128, 4], F32)
    den = K.sb("den", [128, 8], F32)
    of = K.sb("of", [128, 8, 64], F32)
    tmp = dict(junk=junk[:], ss=sm["ss"], rstd=sm["rstd"], xn=xn, idt=idt)

    K.dma("sync", gB[:], g[0:1, :].to_broadcast([128, D]), [], ["gB"])
    K.dma("sync", idt[:], ident[:, :], [], ["idt"])
    K.dma("sync", qg8[:], qgain[0:1, :].to_broadcast([128, 64]), [], ["qg8"])
    K.vec("tensor_scalar_mul", ["qg8"], ["qg8"], out=qg8[:], in0=qg8[:], scalar1=0.125)
    K.dma("sync", kgB[:], kgain[0:1, :].to_broadcast([128, 64]), [], ["kgB"])
    K.dma("sync", sinkE[:], sinks[0:1, :].to_broadcast([128, 32]), [], ["sinkE"])
    K.act(sinkE[:], sinkE[:], AF.Exp, ["sinkE"], ["sinkE"])
    K.dma("sync", flg[:], flagp[:, :], [], ["flg"])
    K.dma("sync", MBT[:], mbt_in[:, :, :, :], [], ["MBT"])
    for c0 in range(0, 4608, 512):
        K.dma("gpsimd", wbi[:, c0:c0 + 512], w_in[:, c0:c0 + 512], [], ["wbi"])
    for c0 in range(0, D, 512):
        K.dma("gpsimd", wbo[:, c0:c0 + 512], w_out[:, c0:c0 + 512], [], ["wbo"])

    wslot = [0]

    def load_w(src, c0, rkey):
        i = wslot[0] % 2
        wslot[0] += 1
        K.dma("sync", wts[i][:], src[:, c0:c0 + CW].rearrange("(k p) n -> p k n", p=128), [rkey], [("wt", i)])
        return wts[i], ("wt", i)

    def proj(xT, kxT, wt, kw):
        bank, kbk = K.G()
        for k in range(16):
            K.mm(bank[:, 0:CW], xT[:, k, :], wt[:, k, :], k == 0, k == 15, [kxT, kw], [kbk])
        return bank, kbk

    def headnorm(src, ksrc, nh, ssb, kss_, gainB, kg, dst, kdst):
        K.act(sq[:, 0:nh * 64], src, AF.Square, [ksrc], ["sq"])
        K.vec("tensor_reduce", ["sq"], [kss_], out=ssb[:], in_=sq[:, 0:nh * 64].rearrange("p (h d) -> p h d", d=64),
              axis=AX.X, op=ALU.add)
        K.act(ssb[:], ssb[:], AF.Sqrt, [kss_], [kss_], scale=1.0 / 64, bias=EPS)
        K.vec("reciprocal", [kss_], [kss_], out=ssb[:], in_=ssb[:])
        s3 = src.rearrange("p (h d) -> p h d", d=64)
        K.vec("tensor_tensor", [ksrc, kss_], [ksrc], out=s3, in0=s3,
              in1=ssb[:].unsqueeze(2).to_broadcast([128, nh, 64]), op=ALU.mult)
        K.vec("tensor_tensor", [ksrc, kg], [kdst], out=dst.rearrange("p (h d) -> p h d", d=64), in0=s3,
              in1=gainB[:].unsqueeze(1).to_broadcast([128, nh, 64]), op=ALU.mult)

    for j in range(nj):
        K.dma("sync", xt[:], x_prev[j], [], ["xt"])
        K.rmsnorm_T(xt[:], "xt", gB, xnTp, "xnTp", tmp)
        K.dma("sync", xt[:], x_own[j], [], ["xt"])
        K.rmsnorm_T(xt[:], "xt", gB, xnT, "xnT", tmp)
        for ci in range(8):
            wt, kw = load_w(wbi, ci * CW, "wbi")
            bank, kbk = proj(xnT, "xnT", wt, kw)
            K.vec("tensor_copy", [kbk], ["qf"], out=qf[:, ci * CW:(ci + 1) * CW], in_=bank[:, 0:CW])
        wt, kw = load_w(wbi, 2048, "wbi")
        bank, kbk = proj(xnT, "xnT", wt, kw)
        K.vec("tensor_copy", [kbk], [("kf", 1)], out=kf[1][:], in_=bank[:, 0:CW])
        bank, kbk = proj(xnTp, "xnTp", wt, kw)
        K.vec("tensor_copy", [kbk], [("kf", 0)], out=kf[0][:], in_=bank[:, 0:CW])
        wt, kw = load_w(wbi, 2304, "wbi")
        bank, kbk = proj(xnT, "xnT", wt, kw)
        K.vec("tensor_copy", [kbk], [("Vaug", 1)], out=Vaug[1][:, :, 0:64], in_=bank[:, 0:CW].rearrange("p (h d) -> p h d", d=64))
        K.vec("memset", [], [("Vaug", 1)], Vaug[1][:, :, 64:65], 1.0)
        bank, kbk = proj(xnTp, "xnTp", wt, kw)
        K.vec("tensor_scalar_mul", [kbk, "flg"], [("Vaug", 0)], out=Vaug[0][:, :, 0:64],
              in0=bank[:, 0:CW].rearrange("p (h d) -> p h d", d=64), scalar1=flg[:, j:j + 1])
        K.vec("tensor_copy", ["flg"], [("Vaug", 0)], out=Vaug[0][:, :, 64:65],
              in_=flg[:, j:j + 1].unsqueeze(1).to_broadcast([128, 4, 1]))
        for ci in range(8):
            wt, kw = load_w(wbi, 2560 + ci * CW, "wbi")
            bank, kbk = proj(xnT, "xnT", wt, kw)
            K.act(sg[:, ci * CW:(ci + 1) * CW], bank[:, 0:CW], AF.Silu, [kbk], ["sg"])
        headnorm(qf[:], "qf", 32, qss, "qss", qg8, "qg8", Cq[:], "Cq")
        K.transposeN(Cq, "Cq", qT, "qT", 32, 64, idt)
        for i in range(2):
            headnorm(kf[i][:], ("kf", i), 4, kss, "kss", kgB, "kgB", Ck[i][:], ("Ck", i))
            K.transposeN(Ck[i], ("Ck", i), kT[i], ("kT", i), 4, 64, idt)
        li = 0
        for gk in range(4):
            for half in range(2):
                L = Lb[li % 2]
                kL = ("L", li % 2)
                s_ = S[li % 2]
                kS = ("S", li % 2)
                pt = PT[li % 2]
                kpt = ("PT", li % 2)
                li += 1
                for n in range(2):
                    h0 = gk * 8 + n * 4
                    K.mm(L[:, n * 512:(n + 1) * 512], kT[half][:, gk, :], qT[:, h0:h0 + 4, :], True, True,
                         [("kT", half), "qT"], [kL])
                K.vec("tensor_tensor", [kL, "MBT"], [kS], out=s_[:], in0=L[:],
                      in1=MBT[:, half, gk * 8:(gk + 1) * 8, :].rearrange("p h q -> p (h q)"), op=ALU.add)
                K.act(pt[:], s_[:], AF.Exp, [kS], [kpt])
                for i8 in range(8):
                    K.mm(Ob[:, i8, 0:65], pt[:, i8 * 128:(i8 + 1) * 128], Vaug[half][:, gk, :],
                         (half == 0 and i8 % 4 == 0), half == 1, [kpt, ("Vaug", half)], ["O"])
            K.vec("tensor_tensor", ["O", "sinkE"], ["den"], out=den[:], in0=Ob[:, :, 64], in1=sinkE[:, gk * 8:(gk + 1) * 8],
                  op=ALU.add)
            K.vec("reciprocal", ["den"], ["den"], out=den[:], in_=den[:])
            K.vec("tensor_tensor", ["O", "den"], ["of"], out=of[:], in0=Ob[:, :, 0:64],
                  in1=den[:].unsqueeze(2).to_broadcast([128, 8, 64]), op=ALU.mult)
            K.vec("tensor_tensor", ["of", "sg"], ["xn"], out=xn[:, gk * 512:(gk + 1) * 512],
                  in0=of[:].rearrange("p h d -> p (h d)"), in1=sg[:, gk * 512:(gk + 1) * 512], op=ALU.mult)
        K.transposeN(xn, "xn", xnT, "xnT", 16, 128, idt)
        for n in range(D // CW):
            wt, kw = load_w(wbo, n * CW, "wbo")
            bank, kbk = proj(xnT, "xnT", wt, kw)
            K.vec("tensor_tensor", [kbk, "xt"], ["xt"], out=xt[:, n * CW:(n + 1) * CW], in0=bank[:, 0:CW],
                  in1=xt[:, n * CW:(n + 1) * CW], op=ALU.add)
        K.dma_out("sync", y[j], xt[:], ["xt"], "y")
    return K.finish()


def l3_consts(rel_bias):
    s = np.arange(128)[:, None]
    q = np.arange(128)[None, :]
    mbt = np.zeros((128, 2, 32, 128), np.float32)
    for half in range(2):
        dist = 128 * (1 - half) + q - s
        ok = (dist >= 0) & (dist < 128)
        vals = rel_bias[bucket_np(np.maximum(dist, 0))]
        vals = np.where(ok[:, :, None], vals, np.float32(MASKV))
        mbt[:, half] = vals.transpose(0, 2, 1)
    return mbt


def l3_inputs(c, x1, mbt, odd_norm, odd_w_in, odd_q_gain, odd_k_gain, odd_sinks, odd_w_out):
    bf = ml_dtypes.bfloat16
    xb = x1.reshape(64, 128, D)
    blks = blocks_of(c)
    x_prev = np.stack([xb[b - 1] if b > 0 else np.zeros((128, D), np.float32) for b in blks])
    flagp = np.zeros((128, 8), np.float32)
    for j, b in enumerate(blks):
        flagp[:, j] = 1.0 if b > 0 else 0.0
    return {"x_own": np.ascontiguousarray(xb[blks]), "x_prev": np.ascontiguousarray(x_prev), "g": odd_norm[0:1],
            "w_in": odd_w_in[0], "qgain": odd_q_gain[0:1], "kgain": odd_k_gain[0:1], "sinks": odd_sinks[0:1],
            "w_out": odd_w_out[0], "mbt": mbt, "flagp": flagp,
            "ident": np.eye(128, dtype=np.float32).astype(bf)}


def kernel(x, rel_bias, even_norm, even_w_in, even_pool_w, even_pool_scale, even_q_gain, even_k_gain, even_w_out,
           odd_norm, odd_w_in, odd_q_gain, odd_k_gain, odd_sinks, odd_w_out):
    f = lambda a: np.ascontiguousarray(np.asarray(a, dtype=np.float32))
    x, rel_bias, even_norm, even_w_in, even_pool_w, even_pool_scale = map(f, (x, rel_bias, even_norm, even_w_in, even_pool_w, even_pool_scale))
    even_q_gain, even_k_gain, even_w_out, odd_norm, odd_w_in = map(f, (even_q_gain, even_k_gain, even_w_out, odd_norm, odd_w_in))
    odd_q_gain, odd_k_gain, odd_sinks, odd_w_out = map(f, (odd_q_gain, odd_k_gain, odd_sinks, odd_w_out))
    cores = list(range(8))
    kT, ikT, v = run_l1(x, even_norm, even_w_in, even_k_gain)
    nc2 = build_l2(8)
    in2 = [l2_inputs(c, x, rel_bias, even_norm, even_w_in, even_pool_w, even_pool_scale, even_q_gain, even_w_out,
                     kT, ikT, v) for c in cores]
    r2 = run_bass_kernel_spmd(nc2, in2, core_ids=cores)
    x1 = np.zeros((64, 128, D), np.float32)
    for c in cores:
        x1[blocks_of(c)] = r2.results[c]["x1"]
    nc3 = build_l3(8)
    mbt = l3_consts(rel_bias)
    in3 = [l3_inputs(c, x1, mbt, odd_norm, odd_w_in, odd_q_gain, odd_k_gain, odd_sinks, odd_w_out) for c in cores]
    r3 = run_bass_kernel_spmd(nc3, in3, core_ids=cores)
    y = np.zeros((64, 128, D), np.float32)
    for c in cores:
        y[blocks_of(c)] = r3.results[c]["y"]
    return y.reshape(1, 8192, D)
```

```python
import numpy as np
import ml_dtypes
from contextlib import ExitStack
import concourse.bass as bass
import concourse.mybir as mybir
from concourse.bass_utils import run_bass_kernel_spmd

F32 = mybir.dt.float32
BF16 = mybir.dt.bfloat16
U32 = mybir.dt.uint32
AF = mybir.ActivationFunctionType
ALU = mybir.AluOpType
AX = mybir.AxisListType
NEG = -1.0e30
D = 2048
EPS = 1e-6
QUEUES = ("sync", "scalar", "vector", "gpsimd", "tensor")


class Prog:
    def __init__(self, nc):
        self.nc = nc
        self.ops = []
        self.res = {}
        self.bar = None

    def op(self, eng, fn, reads=(), writes=(), dma=False, waw=False):
        i = len(self.ops)
        deps = set()
        for r in reads:
            st = self.res.setdefault(r, [[], []])
            deps.update(st[0])
            st[1].append(i)
        for w in writes:
            st = self.res.setdefault(w, [[], []])
            if st[1]:
                deps.update(st[1])
                deps.update(st[0])
                st[0] = [i]
                st[1] = []
            elif waw:
                deps.update(st[0])
                st[0] = [i]
            else:
                st[0].append(i)
        if self.bar is not None:
            deps.add(self.bar)
        deps.discard(i)
        self.ops.append(dict(eng=eng, fn=fn, deps=deps, dma=dma, sig=False,
                             key=(writes[0] if (dma and writes) else None)))
        return i

    def barrier(self):
        self.bar = None
        keys = list(self.res.keys())
        i = self.op("vector", lambda e: e.nop(), reads=(), writes=keys, waw=True)
        self.bar = i

    def emit(self, stack):
        nc = self.nc
        ops = self.ops
        for o in ops:
            keep = set()
            for j in o["deps"]:
                d = ops[j]
                if (not d["dma"]) and (not o["dma"]) and d["eng"] == "tensor" and o["eng"] == "tensor":
                    continue
                keep.add(j)
                d["sig"] = True
            o["deps"] = keep
        cnt = {e: 0 for e in QUEUES}
        dcnt = {}
        for o in ops:
            if not o["sig"]:
                continue
            if o["dma"]:
                k = o["key"]
                dcnt[k] = dcnt.get(k, 0) + 16
                o["sem"] = ("dma", k)
                o["val"] = dcnt[k]
            else:
                cnt[o["eng"]] += 1
                o["sem"] = ("eng", o["eng"])
                o["val"] = cnt[o["eng"]]
        sems = {}
        for o in ops:
            if o["sig"] and o["sem"] not in sems:
                sems[o["sem"]] = stack.enter_context(nc.semaphore("s%d" % len(sems)))
        self.nsem = len(sems)
        block = stack.enter_context(nc.Block())
        per = {e: [o for o in ops if o["eng"] == e] for e in QUEUES}

        def run(e, lst):
            waited = {}
            for o in lst:
                need = {}
                for j in o["deps"]:
                    d = ops[j]
                    need[d["sem"]] = max(need.get(d["sem"], 0), d["val"])
                for s, v in need.items():
                    if waited.get(s, 0) >= v:
                        continue
                    e.wait_ge(sems[s], v)
                    waited[s] = v
                ins = o["fn"](e)
                if o["sig"]:
                    ins.then_inc(sems[o["sem"]], 16 if o["dma"] else 1)

        if per["sync"]:
            @block.sync
            def _(e):
                run(e, per["sync"])
        if per["scalar"]:
            @block.scalar
            def _(e):
                run(e, per["scalar"])
        if per["vector"]:
            @block.vector
            def _(e):
                run(e, per["vector"])
        if per["gpsimd"]:
            @block.gpsimd
            def _(e):
                run(e, per["gpsimd"])
        if per["tensor"]:
            @block.tensor
            def _(e):
                run(e, per["tensor"])


class KB:
    def __init__(self):
        self.nc = bass.Bass("TRN2", target_bir_lowering=False)
        self.st = ExitStack()
        self.P = Prog(self.nc)
        self.outkeys = []
        self.banks = None
        self.gi = 0

    def sb(self, name, shape, dt):
        return self.st.enter_context(self.nc.sbuf_tensor(name, shape, dt))

    def arena_init(self, nbytes):
        self.arena = self.sb("arena", [128, nbytes // 2], BF16)
        self.aoff = 0
        self.asize = nbytes

    def ar(self, name, shape, dt):
        esz = 4 if dt in (F32, U32) else 2
        n = 1
        for s in shape[1:]:
            n *= s
        nb = (n * esz + 31) // 32 * 32
        assert self.aoff + nb <= self.asize, ("arena overflow", name, self.aoff, nb, self.asize)
        v = self.arena[0:shape[0], self.aoff // 2:(self.aoff + n * esz) // 2]
        self.aoff += nb
        if esz == 4:
            v = v.bitcast(dt)
        if len(shape) == 3:
            v = v.rearrange("p (a b) -> p a b", a=shape[1])
        elif len(shape) == 4:
            v = v.rearrange("p (a b c) -> p a b c", a=shape[1], b=shape[2])
        return v

    def arena_reset(self):
        self.P.barrier()
        self.aoff = 0

    def din(self, name, shape, dt):
        return self.nc.dram_tensor(name, list(shape), dt, kind="ExternalInput").ap()

    def dout(self, name, shape, dt):
        return self.nc.dram_tensor(name, list(shape), dt, kind="ExternalOutput").ap()

    def dscr(self, name, shape, dt):
        return self.nc.dram_tensor(name, list(shape), dt, kind="Internal").ap()

    def alloc_banks(self):
        self.banks = [self.st.enter_context(self.nc.psum_tensor("bank%d" % i, [128, 512], F32)) for i in range(8)]

    def G(self):
        gl = getattr(self, "glist", None)
        if gl:
            i = self.gi % len(gl)
            self.gi += 1
            return gl[i]
        i = self.gi % 2
        self.gi += 1
        return self.banks[i], ("bank", i)

    def dma(self, q, out, in_, r, w):
        self.P.op(q, lambda e: e.dma_start(out=out, in_=in_), reads=r, writes=w, dma=True)

    def dma_out(self, q, out, in_, r, key):
        self.P.op(q, lambda e: e.dma_start(out=out, in_=in_), reads=r, writes=[key], dma=True)
        if key not in self.outkeys:
            self.outkeys.append(key)

    def act(self, out, in_, func, r, w, **kw):
        self.P.op("scalar", lambda e: e.activation(out=out, in_=in_, func=func, **kw), reads=r, writes=w)

    def vec(self, name, r, w, *a, **kw):
        self.P.op("vector", lambda e: getattr(e, name)(*a, **kw), reads=r, writes=w)

    def pool(self, name, r, w, *a, **kw):
        self.P.op("gpsimd", lambda e: getattr(e, name)(*a, **kw), reads=r, writes=w)

    def mm(self, out, lhsT, rhs, start, stop, r, w):
        self.P.op("tensor", lambda e: e.matmul(out, lhsT=lhsT, rhs=rhs, start=start, stop=stop), reads=r, writes=w)

    def tr(self, out, in_, ident, r, w):
        self.P.op("tensor", lambda e: e.transpose(out=out, in_=in_, identity=ident), reads=r, writes=w)

    def finish(self):
        self.P.op("sync", lambda e: e.nop(), reads=list(self.outkeys))
        self.P.emit(self.st)
        self.st.close()
        return self.nc

    def rmsnorm_T(self, xt, kx, gB, xnT, kxnT, tmp):
        junk, ss, rstd, xn, idt = tmp["junk"], tmp["ss"], tmp["rstd"], tmp["xn"], tmp["idt"]
        kj = tmp.get("kjunk", "junk")
        kxn = tmp.get("kxn", "xn")
        ks = tmp.get("kss", "ss")
        kr = tmp.get("krstd", "rstd")
        self.act(junk, xt, AF.Square, [kx], [kj, ks], accum_out=ss[:])
        self.act(rstd[:], ss[:], AF.Sqrt, [ks], [kr], scale=1.0 / D, bias=EPS)
        self.vec("reciprocal", [kr], [kr], out=rstd[:], in_=rstd[:])
        self.vec("scalar_tensor_tensor", [kx, kr, "gB"], [kxn], out=xn[:], in0=xt, scalar=rstd[:, 0:1],
                 in1=gB[:], op0=ALU.mult, op1=ALU.mult)
        self.transposeN(xn, kxn, xnT, kxnT, 16, 128, idt)

    def transposeN(self, src, ksrc, dst, kdst, n, width, idt, rows=128):
        per = max(1, 1024 // rows)
        per = min(per, 8)
        i = 0
        while i < n:
            m = min(per, n - i)
            bank, kb = self.G()
            pv = bank[:].bitcast(BF16)[0:width, 0:m * rows].rearrange("p (a b) -> p a b", a=m)
            for a in range(m):
                self.tr(pv[:, a, :], src[0:rows, (i + a) * width:(i + a + 1) * width], idt[0:rows, 0:rows],
                        [ksrc, "idt"], [kb])
            self.vec("tensor_copy", [kb], [kdst], out=dst[0:width, i:i + m, 0:rows], in_=pv)
            i += m

RNG = 32.0
NIT = 30
MASKV = -30000.0
CW = 256
RUN = 4
NRUN = 8 // RUN
NE = NRUN * (RUN + 1)
NSLOT = 64
NDUM = 7 * RUN
EVEN_CHUNKS = [("pin", 0, 4), ("pgate", 1024, 4), ("dq", 2048, 4), ("dgate", 3200, 4), ("iq", 4224, 4)]


def entries():
    out = []
    for m in range(NRUN):
        for i in range(-1, RUN):
            out.append((m * (RUN + 1) + i + 1, m, i))
    return out


def nkb_of(m, i):
    return 8 * RUN * m + 7 * RUN + i + 1


def block_of(c, m, i):
    return RUN * (8 * m + c) + i


def build_fused():
    K = KB()
    nc = K.nc
    xs = K.din("xs", [NSLOT, 128, D], F32)
    xq_own = K.din("xq_own", [NE, 128, D], F32)
    xq_prev = K.din("xq_prev", [NE, 128, D], F32)
    g0 = K.din("g0", [1, D], F32)
    w_in0 = K.din("w_in0", [D, 5328], F32)
    wkv0 = K.din("wkv0", [D, 192], F32)
    pool_w = K.din("pool_w", [4, 256, 256], F32)
    pscale = K.din("pscale", [1, 1024], F32)
    qgain0 = K.din("qgain0", [1, 64], F32)
    kgain0 = K.din("kgain0", [1, 64], F32)
    w_out0 = K.din("w_out0", [D, D], F32)
    U = K.din("U", [128, 16, 1152], F32)
    rb31 = K.din("rb31", [1, 16], F32)
    ident = K.din("ident", [128, 128], BF16)
    i4 = K.din("i4", [128, 512], BF16)
    diagm = K.din("diagm", [128, 128], F32)
    vadd0 = K.din("vadd0", [1, NDUM * 128], BF16)
    vadd1 = K.din("vadd1", [1, NDUM * 128], BF16)
    ktin = K.din("kt", [128, NE], F32)
    poolA = K.din("poolA", [NE, 2, 128, 512], BF16)
    invcnt = K.din("invcnt", [NE, 1, 512], F32)
    g1 = K.din("g1", [1, D], F32)
    w_in1 = K.din("w_in1", [D, 4608], F32)
    qgain1 = K.din("qgain1", [1, 64], F32)
    kgain1 = K.din("kgain1", [1, 64], F32)
    sinks = K.din("sinks", [1, 32], F32)
    w_out1 = K.din("w_out1", [D, D], F32)
    mbt_in = K.din("mbt", [128, 2, 32, 128], F32)
    flagp = K.din("flagp", [128, NE], F32)
    y = K.dout("y", [8, 128, D], F32)
    W0C = [c0 + ci * CW for (_, c0, n) in EVEN_CHUNKS for ci in range(n)] + [5312]
    W1C = list(range(0, 4608, CW))
    WOC = list(range(0, D, CW))
    wbi0 = K.dscr("wbi0", [len(W0C), 128, 16, CW], BF16)
    wbo0 = K.dscr("wbo0", [len(WOC), 128, 16, CW], BF16)
    wbi1 = K.dscr("wbi1", [len(W1C), 128, 16, CW], BF16)
    wbo1 = K.dscr("wbo1", [len(WOC), 128, 16, CW], BF16)
    bts = K.dscr("bts", [8, 2, 128, 2, 1024], BF16)
    x1s = K.dscr("x1s", [NE, 128, D], F32)

    G = [K.st.enter_context(nc.psum_tensor("G%d" % i, [128, 512], F32)) for i in range(2)]
    K.banks = G
    Lb = [K.st.enter_context(nc.psum_tensor("L%d" % i, [128, 1024], F32)) for i in range(2)]
    Ob = K.st.enter_context(nc.psum_tensor("O", [128, 8, 128], F32))
    idt = K.sb("idt", [128, 128], BF16)
    gB = K.sb("gB", [128, D], F32)
    KIK = K.sb("KIK", [128, NSLOT * 128], BF16)
    Vaug = K.sb("Vaug", [128, NSLOT, 65], BF16)
    K.arena_init(175 * 1024)
    sm = {n: K.sb(n, [128, 1], F32) for n in ["ss", "rstd", "lo", "thr", "cnt"]}

    K.dma("sync", idt[:], ident[:, :], [], ["idt"])
    K.dma("sync", gB[:], g0[0:1, :].to_broadcast([128, D]), [], ["gB"])
    def cast_weights(src, dst, cols, ncol, key):
        for t, c0 in enumerate(cols):
            w_ = min(CW, ncol - c0)
            K.dma("gpsimd", dst[t][:, :, 0:w_], src[:, c0:c0 + w_].rearrange("(k p) n -> p k n", p=128), [], [key])

    cast_weights(w_in0, wbi0, W0C, 5328, "wbi0")

    K.glist = [(G[0], ("bank", 0)), (G[1], ("bank", 1)), (Lb[0][:, 0:512], ("La", 0)), (Lb[0][:, 512:1024], ("La", 1)),
               (Lb[1][:, 0:512], ("La", 2)), (Lb[1][:, 512:1024], ("La", 3))]
    NBA = 4
    kgB = K.ar("kgB", [128, 64], F32)
    wt192 = K.ar("wt192", [128, 16, 192], BF16)
    xts = [K.ar("xts%d" % i, [128, D], F32) for i in range(NBA)]
    junkA = [K.ar("junkA%d" % i, [128, D], BF16) for i in range(NBA)]
    xnA = [K.ar("xnA%d" % i, [128, D], BF16) for i in range(NBA)]
    xnTA = [K.ar("xnTA%d" % i, [128, 16, 128], BF16) for i in range(NBA)]
    ssA = [K.ar("ssA%d" % i, [128, 1], F32) for i in range(NBA)]
    rsA = [K.ar("rsA%d" % i, [128, 1], F32) for i in range(NBA)]
    kssA = [K.ar("kssA%d" % i, [128, 1], F32) for i in range(NBA)]
    krsA = [K.ar("krsA%d" % i, [128, 1], F32) for i in range(NBA)]
    kjunkA = [K.ar("kjunkA%d" % i, [128, 64], F32) for i in range(NBA)]
    kvb = [K.ar("kvb%d" % i, [128, 128], BF16) for i in range(NBA)]
    K.dma("sync", kgB[:], kgain0[0:1, :].to_broadcast([128, 64]), [], ["kgB"])
    K.dma("gpsimd", wt192[:], wkv0.rearrange("(k p) n -> p k n", p=128), [], ["wt192"])
    K.vec("memset", [], ["Vaug"], Vaug[:, :, 64:65], 1.0)
    def stage1(u):
        i = u % NBA
        xt_ = xts[i]
        tmpA = dict(junk=junkA[i], kjunk=("junkA", i), ss=ssA[i], kss=("ssA", i), rstd=rsA[i], krstd=("rsA", i),
                    xn=xnA[i], kxn=("xnA", i), idt=idt)
        K.dma("sync", xt_[:], xs[u], [], [("xts", i)])
        K.rmsnorm_T(xt_[:], ("xts", i), gB, xnTA[i], ("xnTA", i), tmpA)

    def stage2(u):
        i = u % NBA
        bank, kbk = K.G()
        for k in range(16):
            K.mm(bank[:, 0:192], xnTA[i][:, k, :], wt192[:, k, :], k == 0, k == 15, [("xnTA", i), "wt192"], [kbk])
        kss, krs, kjunk = kssA[i], krsA[i], kjunkA[i]
        K.act(kjunk[:], bank[:, 64:128], AF.Square, [kbk], [("kjunkA", i), ("kssA", i)], accum_out=kss[:])
        K.act(krs[:], kss[:], AF.Sqrt, [("kssA", i)], [("krsA", i)], scale=1.0 / 64, bias=EPS)
        K.vec("reciprocal", [("krsA", i)], [("krsA", i)], out=krs[:], in_=krs[:])
        K.vec("scalar_tensor_tensor", [kbk, ("krsA", i), "kgB"], [("kvb", i)], out=kvb[i][:, 64:128], in0=bank[:, 64:128],
              scalar=krs[:, 0:1], in1=kgB[:], op0=ALU.mult, op1=ALU.mult)
        K.vec("tensor_copy", [kbk], [("kvb", i)], out=kvb[i][:, 0:64], in_=bank[:, 0:64])
        K.vec("tensor_copy", [kbk], ["Vaug"], out=Vaug[:, u, 0:64], in_=bank[:, 128:192])
        bank2, kb2 = K.G()
        pv = bank2[:].bitcast(BF16)[:, 0:128]
        K.tr(pv, kvb[i][:, :], idt[:], [("kvb", i), "idt"], [kb2])
        K.vec("tensor_copy", [kb2], ["KIK"], out=KIK[:, u * 128:(u + 1) * 128], in_=pv)

    stage1(0)
    stage1(1)
    for u in range(NSLOT):
        stage2(u)
        if u + 2 < NSLOT:
            stage1(u + 2)
    K.arena_reset()
    K.glist = None
    cast_weights(w_out0, wbo0, WOC, D, "wbo0")
    cast_weights(w_in1, wbi1, W1C, 4608, "wbi1")
    cast_weights(w_out1, wbo1, WOC, D, "wbo1")

    I4 = K.ar("I4", [128, 512], BF16)
    score = K.ar("score", [128, 8192], F32)
    mb = K.ar("mb", [128, 8192], BF16)
    junk = K.ar("junk", [128, D], BF16)
    xtb = [K.ar("xt%d" % i, [128, D], F32) for i in range(2)]
    xnb = [K.ar("xn%d" % i, [128, D], BF16) for i in range(2)]
    xnT = K.ar("xnT", [128, 16, 128], BF16)
    yT = K.ar("yT", [128, 16, 128], BF16)
    xnTp = yT
    wts = [K.ar("wt%d" % i, [128, 16, CW], BF16) for i in range(2)]
    pa_own = K.ar("pa_own", [128, 1024], BF16)
    pa_prev = K.ar("pa_prev", [128, 1024], BF16)
    sgp = K.ar("sgp", [128, 1024], F32)
    dqf = K.ar("dqf", [128, 1024], F32)
    sgdb = [K.ar("sgd%d" % i, [128, 1024], F32) for i in range(2)]
    iwc = K.ar("iwc", [128, 16], F32)
    C = K.ar("C", [128, 16, 128], BF16)
    QIQb = [K.ar("QIQ%d" % i, [128, 16, 128], BF16) for i in range(2)]
    rb = [K.ar("r%d" % i, [128, 512], F32) for i in range(4)]
    PT = [K.ar("PT%d" % i, [128, 1024], BF16) for i in range(2)]
    bhl = [K.ar("bhl%d" % i, [128, 2, 1024], BF16) for i in range(2)]
    rb31B = K.ar("rb31B", [128, 16], F32)
    pooledT = K.ar("pooledT", [128, 8, 128], BF16)
    pA = K.ar("pA", [128, 2, 512], BF16)
    invc = K.ar("invc", [128, 512], F32)
    pw = K.ar("pw", [128, 4, 2, 256], BF16)
    qg8 = K.ar("qg8", [128, 64], F32)
    vaddB = K.ar("vaddB", [128, NDUM * 128], BF16)
    dm = K.ar("dm", [128, 128], F32)
    ktc = K.ar("ktc", [128, NE], F32)
    qss = K.ar("qss", [128, 16], F32)
    qrs = K.ar("qrs", [128, 16], F32)
    rden = K.ar("rden", [128, 8], F32)
    of = K.ar("of", [128, 8, 64], F32)
    ssB = K.ar("ssB", [128, 1], F32)
    rstdB = K.ar("rstdB", [128, 1], F32)
    SK = [("score", c) for c in range(16)]
    bstage = sgp.rearrange("p (a b) -> p a b", a=8)
    bstage2 = dqf.rearrange("p (a b) -> p a b", a=8)

    K.dma("sync", I4[:], i4[:, :], [], ["I4"])
    K.dma("sync", rb31B[:], rb31[0:1, :].to_broadcast([128, 16]), [], ["rb31B"])
    K.dma("sync", qg8[:], qgain0[0:1, :].to_broadcast([128, 64]), [], ["qg8"])
    K.vec("tensor_scalar_mul", ["qg8"], ["qg8"], out=qg8[:], in0=qg8[:], scalar1=0.125)
    K.dma("sync", vaddB[:], vadd0[0:1, :].to_broadcast([128, NDUM * 128]), [], ["vaddB"])
    K.dma("sync", dm[:], diagm[:, :], [], ["dm"])
    K.dma("sync", ktc[:], ktin[:, :], [], ["ktc"])
    pwf = score[:, 0:2048].rearrange("p (g k d) -> p g k d", g=4, k=2)
    K.dma("sync", pwf, pool_w.rearrange("g (k p) d -> p g k d", p=128), [], [SK[0]])
    K.dma("sync", score[:, 2048:3072], pscale[0:1, :].to_broadcast([128, 1024]), [], [SK[1]])
    for gg in range(4):
        K.vec("tensor_tensor", [SK[0], SK[1]], ["pw"], out=pw[:, gg, :, :], in0=pwf[:, gg, :, :],
              in1=score[:, 2048 + gg * 256:2048 + (gg + 1) * 256].unsqueeze(1).to_broadcast([128, 2, 256]), op=ALU.mult)
    for dl in range(8):
        for hh in range(2):
            hs = slice(hh * 8, hh * 8 + 8)
            K.dma("sync", bstage, U[:, hs, dl * 128:(dl + 1) * 128], [], ["sgp"])
            K.vec("tensor_tensor", ["sgp", "rb31B"], ["sgp"], out=bstage, in0=bstage,
                  in1=rb31B[:, hs].unsqueeze(2).to_broadcast([128, 8, 128]), op=ALU.subtract)
            i = (dl * 2 + hh) % 2
            bv = bhl[i]
            kb_ = ("bhl", i)
            K.vec("tensor_copy", ["sgp"], [kb_], out=bv[:, 0, :], in_=sgp[:])
            K.vec("tensor_tensor", ["sgp", kb_], ["dqf"], out=dqf[:], in0=sgp[:], in1=bv[:, 0, :], op=ALU.subtract)
            K.vec("tensor_copy", ["dqf"], [kb_], out=bv[:, 1, :], in_=dqf[:])
            K.dma("sync", bts[dl, hh], bv[:], [kb_], [("bts", i)])

    wslot = [0]

    wtl = [wts]

    def load_w(src, cols, c0, rkey):
        nb = len(wtl[0])
        i = wslot[0] % nb
        wslot[0] += 1
        K.dma("sync", wtl[0][i][:], src[cols.index(c0)], [rkey], [("wt", i)])
        return wtl[0][i], ("wt", i)

    ents = entries()

    def geom(e):
        _, m, ii = ents[e]
        nkb = nkb_of(m, ii)
        N = 128 * nkb
        chunks = [(c0, min(512, N - c0)) for c0 in range(0, N, 512)]
        return nkb, N, chunks

    def front(e):
        p = e % 2
        nkb, N, chunks = geom(e)
        nch = len(chunks)
        xt, xn, sgd, QIQ = xtb[p], xnb[p], sgdb[p], QIQb[p]
        kxt, kxn, ksgd, kQ = ("xt", p), ("xn", p), ("sgd", p), ("QIQ", p)
        tmp = dict(junk=junk, kjunk="junk", ss=ssB, rstd=rstdB, xn=xn, kxn=kxn, idt=idt)
        if e == 1:
            K.dma("sync", vaddB[:], vadd1[0:1, :].to_broadcast([128, NDUM * 128]), [], ["vaddB"])
        K.dma("gpsimd", xt[:], xq_prev[e], [], [kxt])
        K.rmsnorm_T(xt[:], kxt, gB, xnTp, "yT", tmp)
        K.dma("gpsimd", xt[:], xq_own[e], [], [kxt])
        K.rmsnorm_T(xt[:], kxt, gB, xnT, "xnT", tmp)
        K.dma("gpsimd", pA[:], poolA[e].rearrange("a p n -> p a n"), [], ["pA"])
        K.dma("gpsimd", invc[:], invcnt[e, 0:1, :].to_broadcast([128, 512]), [], ["invc"])

        def proj_chunk(kind, col0, ci):
            c0 = col0 + ci * CW
            wt, kw = load_w(wbi0, W0C, c0, "wbi0")
            bank, kbk = K.G()
            for k in range(16):
                K.mm(bank[:, 0:CW], xnT[:, k, :], wt[:, k, :], k == 0, k == 15, ["xnT", kw], [kbk])
            lo_, hi_ = ci * CW, (ci + 1) * CW
            if kind == "pin":
                K.vec("tensor_copy", [kbk], ["pa_own"], out=pa_own[:, lo_:hi_], in_=bank[:, 0:CW])
                bank2, kb2 = K.G()
                for k in range(16):
                    K.mm(bank2[:, 0:CW], xnTp[:, k, :], wt[:, k, :], k == 0, k == 15, ["yT", kw], [kb2])
                K.vec("tensor_copy", [kb2], ["pa_prev"], out=pa_prev[:, lo_:hi_], in_=bank2[:, 0:CW])
            elif kind == "pgate":
                K.act(sgp[:, lo_:hi_], bank[:, 0:CW], AF.Silu, [kbk], ["sgp"])
            elif kind == "dq":
                K.vec("tensor_copy", [kbk], ["dqf"], out=dqf[:, lo_:hi_], in_=bank[:, 0:CW])
            elif kind == "dgate":
                K.act(sgd[:, lo_:hi_], bank[:, 0:CW], AF.Silu, [kbk], [ksgd])
            elif kind == "iq":
                K.vec("tensor_copy", [kbk], ["C"], out=C[:, ci * 4:(ci + 1) * 4, 0:64],
                      in_=bank[:, 0:CW].rearrange("p (h d) -> p h d", d=64))

        cmap = {kind: col0 for (kind, col0, n) in EVEN_CHUNKS}
        for kind in ("iq", "dq"):
            for ci in range(4):
                proj_chunk(kind, cmap[kind], ci)
        i = wslot[0] % 2
        wslot[0] += 1
        K.dma("sync", wts[i][:, :, 0:16], wbi0[W0C.index(5312)][:, :, 0:16], ["wbi0"], [("wt", i)])
        bank, kbk = K.G()
        for k in range(16):
            K.mm(bank[:, 0:16], xnT[:, k, :], wts[i][:, k, 0:16], k == 0, k == 15, ["xnT", ("wt", i)], [kbk])
        K.vec("tensor_scalar_mul", [kbk], ["iwc"], out=iwc[:], in0=bank[:, 0:16], scalar1=1.0 / 32.0)
        K.act(score[:, 0:1024], dqf[:], AF.Square, ["dqf"], [SK[0], SK[1]])
        K.vec("tensor_reduce", [SK[0], SK[1]], ["qss"], out=qss[:], in_=score[:, 0:1024].rearrange("p (h d) -> p h d", d=64),
              axis=AX.X, op=ALU.add)
        K.act(qrs[:], qss[:], AF.Sqrt, ["qss"], ["qrs"], scale=1.0 / 64, bias=EPS)
        K.vec("reciprocal", ["qrs"], ["qrs"], out=qrs[:], in_=qrs[:])
        dq3 = dqf[:].rearrange("p (h d) -> p h d", d=64)
        K.vec("tensor_tensor", ["dqf", "qrs"], ["dqf"], out=dq3, in0=dq3,
              in1=qrs[:].unsqueeze(2).to_broadcast([128, 16, 64]), op=ALU.mult)
        K.vec("tensor_tensor", ["dqf", "qg8"], ["C"], out=C[:, :, 64:128], in0=dq3,
              in1=qg8[:].unsqueeze(1).to_broadcast([128, 16, 64]), op=ALU.mult)
        K.transposeN(C[:].rearrange("p h d -> p (h d)"), "C", QIQ, kQ, 16, 128, idt)
        tasks = [(kind, cmap[kind], ci) for kind in ("pin", "pgate", "dgate") for ci in range(4)]
        ibanks = [(G[0][:, :], ("bank", 0)), (G[1][:, :], ("bank", 1)), (Lb[0][:, 0:512], ("L", 0)), (Lb[1][:, 0:512], ("L", 1))]
        nunits = nch * 16
        stride = max(1, nunits // (len(tasks) + 1))
        ri = 0
        for ch, (c0, cw) in enumerate(chunks):
            for h in range(16):
                bank, kbk = ibanks[ri % 4]
                K.mm(bank[:, 0:cw], QIQ[0:64, h, :], KIK[0:64, c0:c0 + cw], True, True, [kQ, "KIK"], [kbk])
                r = rb[ri % len(rb)]
                kr = ("r", ri % len(rb))
                ri += 1
                K.act(r[:, 0:cw], bank[:, 0:cw], AF.Relu, [kbk], [kr])
                sc = score[:, c0:c0 + cw]
                if h == 0:
                    K.vec("tensor_scalar_mul", [kr, "iwc"], [SK[ch]], out=sc, in0=r[:, 0:cw], scalar1=iwc[:, 0:1])
                else:
                    K.vec("scalar_tensor_tensor", [kr, "iwc", SK[ch]], [SK[ch]], out=sc, in0=r[:, 0:cw], scalar=iwc[:, h:h + 1],
                          in1=sc, op0=ALU.mult, op1=ALU.add)
                if tasks and ri % stride == 0:
                    proj_chunk(*tasks.pop(0))
        while tasks:
            proj_chunk(*tasks.pop(0))
        nv = min(N, NDUM * 128)
        vk = SK[:(nv + 511) // 512]
        K.vec("tensor_tensor", vk + ["vaddB"], vk, out=score[:, 0:nv], in0=score[:, 0:nv], in1=vaddB[:, 0:nv], op=ALU.add)
        K.vec("tensor_tensor", [SK[nch - 1], "dm"], [SK[nch - 1]], out=score[:, N - 128:N], in0=score[:, N - 128:N],
              in1=dm[:], op=ALU.add)
        for half in range(2):
            bank, kbk = K.G()
            for gi in range(2):
                gg = half * 2 + gi
                for kc in range(2):
                    cs = slice(gg * 256 + kc * 128, gg * 256 + (kc + 1) * 128)
                    o_ = bank[:, (gi * 2 + kc) * 128:(gi * 2 + kc + 1) * 128]
                    K.mm(o_, pa_prev[:, cs], pA[:, 0, gg * 128:(gg + 1) * 128], True, False, ["pa_prev", "pA"], [kbk])
                    K.mm(o_, pa_own[:, cs], pA[:, 1, gg * 128:(gg + 1) * 128], False, True, ["pa_own", "pA"], [kbk])
            K.vec("tensor_tensor", [kbk, "invc"], ["pooledT"],
                  out=pooledT[:, half * 4:(half + 1) * 4, :].rearrange("p (g k) t -> p g k t", g=2),
                  in0=bank[:].rearrange("p (g k t) -> p g k t", g=2, k=2),
                  in1=invc[:, half * 256:(half + 1) * 256].rearrange("p (g t) -> p g t", g=2).unsqueeze(2).to_broadcast([128, 2, 2, 128]),
                  op=ALU.mult)
        for half in range(2):
            bank, kbk = K.G()
            for gi in range(2):
                gg = half * 2 + gi
                for kc in range(2):
                    K.mm(bank[:, gi * 256:(gi + 1) * 256], pooledT[:, gg * 2 + kc, :], pw[:, gg, kc, :], kc == 0, kc == 1,
                         ["pooledT", "pw"], [kbk])
            K.vec("tensor_tensor", [kbk, "sgp"], [kxn], out=xn[:, half * 512:(half + 1) * 512], in0=bank[:],
                  in1=sgp[:, half * 512:(half + 1) * 512], op=ALU.mult)

    lo, thr = sm["lo"], sm["thr"]
    U8 = mybir.dt.uint8
    junk8 = junk.bitcast(U8)
    JW = 4096
    cnt2 = K.ar("cnt2", [128, 2], F32)
    flagf = K.ar("flagf", [128, 1], F32)

    def bisect_iters(e, it0, it1):
        nkb, N, chunks = geom(e)
        SKn = SK[:len(chunks)]
        if it0 == 0:
            K.vec("memset", [], ["lo"], lo[:], -RNG)
            K.vec("memset", [], ["cnt2", ("cnt2", 0), ("cnt2", 1)], cnt2[:], 0.0)
        for it in range(it0, it1):
            step = 2.0 * RNG / (2.0 ** (it + 1))
            K.vec("tensor_scalar_add", ["lo"], ["thr"], out=thr[:], in0=lo[:], scalar1=step)
            for ci, a in enumerate(range(0, N, JW)):
                b = min(N, a + JW)
                K.vec("tensor_scalar", SKn + ["thr"], ["junk", ("cnt2", ci)], out=junk8[:, 0:b - a], in0=score[:, a:b],
                      scalar1=thr[:, 0:1], scalar2=None, op0=ALU.is_ge, op1=ALU.add, accum_out=cnt2[:, ci:ci + 1])
            K.vec("scalar_tensor_tensor", [("cnt2", 0), ("cnt2", 1), "cnt2", "ktc"], ["flagf"], out=flagf[:], in0=cnt2[:, 0:1],
                  scalar=cnt2[:, 1:2], in1=ktc[:, e:e + 1], op0=ALU.add, op1=ALU.is_ge)
            K.vec("scalar_tensor_tensor", ["flagf", "lo"], ["lo"], out=lo[:], in0=flagf[:], scalar=step, in1=lo[:],
                  op0=ALU.mult, op1=ALU.add)

    def mbwrite(e):
        nkb, N, chunks = geom(e)
        SKn = SK[:len(chunks)]
        K.vec("tensor_scalar", SKn + ["lo"], ["mb"], out=mb[:, 0:N], in0=score[:, 0:N], scalar1=lo[:, 0:1],
              scalar2=MASKV, op0=ALU.is_lt, op1=ALU.mult)

    lbi = [0, 0]

    def back_attn(e, hh):
        p = e % 2
        nkb, N, chunks = geom(e)
        xn, sgd, QIQ = xnb[p], sgdb[p], QIQb[p]
        kxn, ksgd, kQ = ("xn", p), ("sgd", p), ("QIQ", p)
        for kb in range(nkb):
            dl = nkb - 1 - kb
            li = lbi[0]
            lbi[0] += 1
            L = Lb[li % 2]
            kL = ("L", li % 2)
            pt = PT[li % 2]
            kpt = ("PT", li % 2)
            near = dl < 8
            if near:
                bi = lbi[1]
                lbi[1] += 1
                bv = bhl[bi % 2]
                kbv = ("bhl", bi % 2)
                K.dma("gpsimd", bv[:], bts[dl, hh], [("bts", 0), ("bts", 1)], [kbv])
            for n in range(2):
                Ls = L[:, n * 512:(n + 1) * 512]
                h0 = hh * 8 + n * 4
                K.mm(Ls, KIK[64:128, kb * 128:(kb + 1) * 128], QIQ[64:128, h0:h0 + 4, :], True, False,
                     ["KIK", kQ], [kL])
                K.mm(Ls, mb[:, kb * 128:(kb + 1) * 128], I4[:], False, not near, ["mb", "I4"], [kL])
                if near:
                    K.mm(Ls, idt[:], bv[:, 0, n * 512:(n + 1) * 512], False, False, ["idt", kbv], [kL])
                    K.mm(Ls, idt[:], bv[:, 1, n * 512:(n + 1) * 512], False, True, ["idt", kbv], [kL])
            K.act(pt[:], L[:], AF.Exp, [kL], [kpt])
            for i8 in range(8):
                K.mm(Ob[:, i8, 0:65], pt[:, i8 * 128:(i8 + 1) * 128], Vaug[:, kb, :], (kb == 0 and i8 % 4 == 0),
                     kb == nkb - 1, [kpt, "Vaug"], ["O"])
        K.vec("reciprocal", ["O"], ["rden"], out=rden[:], in_=Ob[:, :, 64])
        K.vec("tensor_tensor", ["O", "rden"], ["of"], out=of[:], in0=Ob[:, :, 0:64],
              in1=rden[:].unsqueeze(2).to_broadcast([128, 8, 64]), op=ALU.mult)
        K.vec("tensor_tensor", ["of", ksgd], [kxn], out=xn[:, 1024 + hh * 512:1024 + (hh + 1) * 512],
              in0=of[:].rearrange("p h d -> p (h d)"), in1=sgd[:, hh * 512:(hh + 1) * 512], op=ALU.mult)

    def back_out(e):
        p = e % 2
        xt, xn = xtb[p], xnb[p]
        kxt, kxn = ("xt", p), ("xn", p)
        K.transposeN(xn, kxn, yT, "yT", 16, 128, idt)
        for n in range(D // CW):
            wt, kw = load_w(wbo0, WOC, n * CW, "wbo0")
            bank, kbk = K.G()
            for k in range(16):
                K.mm(bank[:, 0:CW], yT[:, k, :], wt[:, k, :], k == 0, k == 15, ["yT", kw], [kbk])
            K.vec("tensor_tensor", [kbk, kxt], [kxt], out=xt[:, n * CW:(n + 1) * CW], in0=bank[:, 0:CW],
                  in1=xt[:, n * CW:(n + 1) * CW], op=ALU.add)
        K.dma("gpsimd", x1s[e], xt[:], [kxt], [("x1s", e)])

    H1 = NIT // 2
    for e in range(NE):
        front(e)
        if e == 0:
            bisect_iters(e, 0, NIT)
        else:
            bisect_iters(e, 0, H1)
            back_attn(e - 1, 0)
            bisect_iters(e, H1, NIT)
            back_attn(e - 1, 1)
            back_out(e - 1)
        mbwrite(e)
    back_attn(NE - 1, 0)
    back_attn(NE - 1, 1)
    back_out(NE - 1)
    K.arena_reset()

    xt = K.ar("xt", [128, D], F32)
    xn = K.ar("xn", [128, D], BF16)
    junk = K.ar("junk", [128, D], BF16)
    xnT = K.ar("xnT", [128, 16, 128], BF16)
    xnTp = K.ar("xnTp", [128, 16, 128], BF16)
    wts = [K.ar("wt%d" % i, [128, 16, CW], BF16) for i in range(4)]
    wtl[0] = wts
    RES1 = [2048, 2304, 0]
    wres = {c0: K.ar("wres%d" % c0, [128, 16, CW], BF16) for c0 in RES1}
    qf = K.ar("qf", [128, D], F32)
    sq = K.ar("sq", [128, D], F32)
    sg = K.ar("sg", [128, D], F32)
    kf = [K.ar("kf%d" % i, [128, 256], F32) for i in range(2)]
    Cq = K.ar("Cq", [128, D], BF16)
    Ck = [K.ar("Ck%d" % i, [128, 256], BF16) for i in range(2)]
    qT = K.ar("qT", [64, 32, 128], BF16)
    kT = [K.ar("kT%d" % i, [64, 4, 128], BF16) for i in range(2)]
    Va = [K.ar("Va%d" % i, [128, 4, 65], BF16) for i in range(2)]
    MBT = K.ar("MBT", [128, 2, 32, 128], F32)
    S = [K.ar("S%d" % i, [128, 1024], F32) for i in range(2)]
    PT = [K.ar("PT%d" % i, [128, 1024], BF16) for i in range(2)]
    qg8 = K.ar("qg8", [128, 64], F32)
    kgB = K.ar("kgB", [128, 64], F32)
    sinkE = K.ar("sinkE", [128, 32], F32)
    flg = K.ar("flg", [128, NE], F32)
    qss = K.ar("qss", [128, 32], F32)
    kss = K.ar("kss", [128, 4], F32)
    den = K.ar("den", [128, 8], F32)
    of = K.ar("of", [128, 8, 64], F32)
    tmp = dict(junk=junk, ss=sm["ss"], rstd=sm["rstd"], xn=xn, idt=idt)

    K.dma("sync", gB[:], g1[0:1, :].to_broadcast([128, D]), [], ["gB"])
    K.dma("sync", qg8[:], qgain1[0:1, :].to_broadcast([128, 64]), [], ["qg8"])
    K.vec("tensor_scalar_mul", ["qg8"], ["qg8"], out=qg8[:], in0=qg8[:], scalar1=0.125)
    K.dma("sync", kgB[:], kgain1[0:1, :].to_broadcast([128, 64]), [], ["kgB"])
    K.dma("sync", sinkE[:], sinks[0:1, :].to_broadcast([128, 32]), [], ["sinkE"])
    K.act(sinkE[:], sinkE[:], AF.Exp, ["sinkE"], ["sinkE"])
    K.dma("sync", flg[:], flagp[:, :], [], ["flg"])
    K.dma("sync", MBT[:], mbt_in[:, :, :, :], [], ["MBT"])
    wslot[0] = 0
    for c0 in RES1:
        K.dma("sync", wres[c0][:], wbi1[W1C.index(c0)], ["wbi1"], [("wres", c0)])
    _load_w = load_w

    def load_w(src, cols, c0, rkey):
        if src is wbi1 and c0 in wres:
            return wres[c0], ("wres", c0)
        return _load_w(src, cols, c0, rkey)

    def proj(xT, kxT, wt, kw):
        bank, kbk = K.G()
        for k in range(16):
            K.mm(bank[:, 0:CW], xT[:, k, :], wt[:, k, :], k == 0, k == 15, [kxT, kw], [kbk])
        return bank, kbk

    def headnorm(src, ksrc, nh, ssb, kss_, gainB, kg, dst, kdst):
        K.act(sq[:, 0:nh * 64], src, AF.Square, [ksrc], ["sq"])
        K.vec("tensor_reduce", ["sq"], [kss_], out=ssb[:], in_=sq[:, 0:nh * 64].rearrange("p (h d) -> p h d", d=64),
              axis=AX.X, op=ALU.add)
        K.act(ssb[:], ssb[:], AF.Sqrt, [kss_], [kss_], scale=1.0 / 64, bias=EPS)
        K.vec("reciprocal", [kss_], [kss_], out=ssb[:], in_=ssb[:])
        s3 = src.rearrange("p (h d) -> p h d", d=64)
        K.vec("tensor_tensor", [ksrc, kss_], [ksrc], out=s3, in0=s3,
              in1=ssb[:].unsqueeze(2).to_broadcast([128, nh, 64]), op=ALU.mult)
        K.vec("tensor_tensor", [ksrc, kg], [kdst], out=dst.rearrange("p (h d) -> p h d", d=64), in0=s3,
              in1=gainB[:].unsqueeze(1).to_broadcast([128, nh, 64]), op=ALU.mult)

    oi = 0
    for (e, m, ii) in entries():
        if ii < 0:
            continue
        K.dma("sync", xt[:], x1s[e - 1], [("x1s", e - 1)], ["xt"])
        K.rmsnorm_T(xt[:], "xt", gB, xnTp, "xnTp", tmp)
        K.dma("sync", xt[:], x1s[e], [("x1s", e)], ["xt"])
        K.rmsnorm_T(xt[:], "xt", gB, xnT, "xnT", tmp)
        for ci in range(8):
            wt, kw = load_w(wbi1, W1C, ci * CW, "wbi1")
            bank, kbk = proj(xnT, "xnT", wt, kw)
            K.vec("tensor_copy", [kbk], ["qf"], out=qf[:, ci * CW:(ci + 1) * CW], in_=bank[:, 0:CW])
        wt, kw = load_w(wbi1, W1C, 2048, "wbi1")
        bank, kbk = proj(xnT, "xnT", wt, kw)
        K.vec("tensor_copy", [kbk], [("kf", 1)], out=kf[1][:], in_=bank[:, 0:CW])
        bank, kbk = proj(xnTp, "xnTp", wt, kw)
        K.vec("tensor_copy", [kbk], [("kf", 0)], out=kf[0][:], in_=bank[:, 0:CW])
        wt, kw = load_w(wbi1, W1C, 2304, "wbi1")
        bank, kbk = proj(xnT, "xnT", wt, kw)
        K.vec("tensor_copy", [kbk], [("Va", 1)], out=Va[1][:, :, 0:64], in_=bank[:, 0:CW].rearrange("p (h d) -> p h d", d=64))
        K.vec("memset", [], [("Va", 1)], Va[1][:, :, 64:65], 1.0)
        bank, kbk = proj(xnTp, "xnTp", wt, kw)
        K.vec("tensor_scalar_mul", [kbk, "flg"], [("Va", 0)], out=Va[0][:, :, 0:64],
              in0=bank[:, 0:CW].rearrange("p (h d) -> p h d", d=64), scalar1=flg[:, e:e + 1])
        K.vec("tensor_copy", ["flg"], [("Va", 0)], out=Va[0][:, :, 64:65],
              in_=flg[:, e:e + 1].unsqueeze(1).to_broadcast([128, 4, 1]))
        for ci in range(8):
            wt, kw = load_w(wbi1, W1C, 2560 + ci * CW, "wbi1")
            bank, kbk = proj(xnT, "xnT", wt, kw)
            K.act(sg[:, ci * CW:(ci + 1) * CW], bank[:, 0:CW], AF.Silu, [kbk], ["sg"])
        headnorm(qf[:], "qf", 32, qss, "qss", qg8, "qg8", Cq[:], "Cq")
        K.transposeN(Cq, "Cq", qT, "qT", 32, 64, idt)
        for i in range(2):
            headnorm(kf[i][:], ("kf", i), 4, kss, "kss", kgB, "kgB", Ck[i][:], ("Ck", i))
            K.transposeN(Ck[i], ("Ck", i), kT[i], ("kT", i), 4, 64, idt)
        li = 0
        for gk in range(4):
            for half in range(2):
                L = Lb[li % 2]
                kL = ("L", li % 2)
                s_ = S[li % 2]
                kS = ("S", li % 2)
                pt = PT[li % 2]
                kpt = ("PT", li % 2)
                li += 1
                for n in range(2):
                    h0 = gk * 8 + n * 4
                    K.mm(L[:, n * 512:(n + 1) * 512], kT[half][:, gk, :], qT[:, h0:h0 + 4, :], True, True,
                         [("kT", half), "qT"], [kL])
                K.vec("tensor_tensor", [kL, "MBT"], [kS], out=s_[:], in0=L[:],
                      in1=MBT[:, half, gk * 8:(gk + 1) * 8, :].rearrange("p h q -> p (h q)"), op=ALU.add)
                K.act(pt[:], s_[:], AF.Exp, [kS], [kpt])
                for i8 in range(8):
                    K.mm(Ob[:, i8, 0:65], pt[:, i8 * 128:(i8 + 1) * 128], Va[half][:, gk, :],
                         (half == 0 and i8 % 4 == 0), half == 1, [kpt, ("Va", half)], ["O"])
            K.vec("tensor_tensor", ["O", "sinkE"], ["den"], out=den[:], in0=Ob[:, :, 64], in1=sinkE[:, gk * 8:(gk + 1) * 8],
                  op=ALU.add)
            K.vec("reciprocal", ["den"], ["den"], out=den[:], in_=den[:])
            K.vec("tensor_tensor", ["O", "den"], ["of"], out=of[:], in0=Ob[:, :, 0:64],
                  in1=den[:].unsqueeze(2).to_broadcast([128, 8, 64]), op=ALU.mult)
            K.vec("tensor_tensor", ["of", "sg"], ["xn"], out=xn[:, gk * 512:(gk + 1) * 512],
                  in0=of[:].rearrange("p h d -> p (h d)"), in1=sg[:, gk * 512:(gk + 1) * 512], op=ALU.mult)
        K.transposeN(xn, "xn", xnT, "xnT", 16, 128, idt)
        for n in range(D // CW):
            wt, kw = load_w(wbo1, WOC, n * CW, "wbo1")
            bank, kbk = proj(xnT, "xnT", wt, kw)
            K.vec("tensor_tensor", [kbk, "xt"], ["xt"], out=xt[:, n * CW:(n + 1) * CW], in0=bank[:, 0:CW],
                  in1=xt[:, n * CW:(n + 1) * CW], op=ALU.add)
        K.dma_out("sync", y[oi], xt[:], ["xt"], "y")
        oi += 1
    return K.finish()


def bucket_np(n):
    import math
    n = np.maximum(n, 0)
    nf = np.maximum(n, 1).astype(np.float32)
    large = 16 + (np.log(nf / np.float32(16)) / np.float32(math.log(64)) * np.float32(16)).astype(np.int32)
    large = np.minimum(large, 31)
    return np.where(n < 16, n, large)


def shared_inputs(rel_bias, even_norm, even_w_in, even_pool_w, even_pool_scale, even_q_gain, even_k_gain, even_w_out,
                  odd_norm, odd_w_in, odd_q_gain, odd_k_gain, odd_sinks, odd_w_out):
    bf = ml_dtypes.bfloat16
    s = np.arange(128)[:, None]
    n = np.arange(1152)[None, :]
    U = np.ascontiguousarray(rel_bias[bucket_np(np.maximum(n - s, 0))][:, :, :16].transpose(0, 2, 1)).astype(np.float32)
    q = np.arange(128)[:, None]
    ss = np.arange(128)[None, :]
    diagm = np.where(ss <= q, 0.0, NEG).astype(np.float32)
    qq = np.arange(128)[None, :]
    mbt = np.zeros((128, 2, 32, 128), np.float32)
    for half in range(2):
        dist = 128 * (1 - half) + qq - s
        ok = (dist >= 0) & (dist < 128)
        vals = rel_bias[bucket_np(np.maximum(dist, 0))]
        vals = np.where(ok[:, :, None], vals, np.float32(MASKV))
        mbt[:, half] = vals.transpose(0, 2, 1)
    w0 = even_w_in[0]
    wkv0 = np.ascontiguousarray(np.concatenate([w0[:, 5248:5312], w0[:, 3072:3136], w0[:, 3136:3200]], axis=1))
    return {
        "g0": even_norm[0:1], "w_in0": w0, "wkv0": wkv0, "pool_w": even_pool_w[0], "pscale": even_pool_scale[0:1],
        "qgain0": even_q_gain[0:1], "kgain0": even_k_gain[0:1], "w_out0": even_w_out[0], "U": U,
        "rb31": np.ascontiguousarray(rel_bias[31:32, :16]),
        "ident": np.eye(128, dtype=np.float32).astype(bf),
        "i4": np.tile(np.eye(128, dtype=np.float32), (1, 4)).astype(bf), "diagm": diagm,
        "g1": odd_norm[0:1], "w_in1": odd_w_in[0], "qgain1": odd_q_gain[0:1], "kgain1": odd_k_gain[0:1],
        "sinks": odd_sinks[0:1], "w_out1": odd_w_out[0], "mbt": mbt,
    }


def core_inputs(c, x):
    bf = ml_dtypes.bfloat16
    xb = x.reshape(64, 128, D)
    zero = np.zeros((128, D), np.float32)
    blk = lambda b: xb[b] if 0 <= b < 64 else zero
    xs = np.stack([blk(u + RUN * c - 7 * RUN) for u in range(NSLOT)])
    ents = entries()
    xq_own = np.stack([blk(block_of(c, m, i)) for (e, m, i) in ents])
    xq_prev = np.stack([blk(block_of(c, m, i) - 1) for (e, m, i) in ents])
    vadd1 = np.zeros((1, NDUM * 128), np.float32)
    for u in range(NDUM):
        if u + RUN * c - 7 * RUN < 0:
            vadd1[0, u * 128:(u + 1) * 128] = NEG
    vadd0 = vadd1.copy() if block_of(c, 0, -1) >= 0 else np.zeros_like(vadd1)
    kt = np.ones((128, NE), np.float32)
    flagp = np.zeros((128, NE), np.float32)
    poolA = np.zeros((NE, 2, 128, 4, 128), np.float32)
    invcnt = np.ones((NE, 1, 4, 128), np.float32)
    for (e, m, i) in ents:
        b = block_of(c, m, i)
        flagp[:, e] = 1.0 if b - 1 >= 0 else 0.0
        bb = max(b, 0)
        tg = bb * 128 + np.arange(128)
        kt[:, e] = np.minimum(256, tg + 1) if b >= 0 else 1.0
        for gi, w in enumerate((2, 4, 8, 16)):
            cnt = np.minimum(w, tg + 1)
            invcnt[e, 0, gi] = 1.0 / cnt
            t = np.arange(128)[None, :]
            s_ = np.arange(128)[:, None]
            own = ((s_ <= t) & (s_ > t - w)).astype(np.float32) - np.where(s_ == t, cnt[None, :], 0)
            prev = ((s_ - 128 > t - w)).astype(np.float32) if b > 0 else np.zeros((128, 128), np.float32)
            poolA[e, 0, :, gi, :] = prev
            poolA[e, 1, :, gi, :] = own
    return {"xs": np.ascontiguousarray(xs), "xq_own": np.ascontiguousarray(xq_own), "xq_prev": np.ascontiguousarray(xq_prev),
            "vadd0": vadd0.astype(bf), "vadd1": vadd1.astype(bf), "kt": kt, "flagp": flagp,
            "poolA": poolA.reshape(NE, 2, 128, 512).astype(bf), "invcnt": invcnt.reshape(NE, 1, 512)}


def kernel(x, rel_bias, even_norm, even_w_in, even_pool_w, even_pool_scale, even_q_gain, even_k_gain, even_w_out,
           odd_norm, odd_w_in, odd_q_gain, odd_k_gain, odd_sinks, odd_w_out):
    f = lambda a: np.ascontiguousarray(np.asarray(a, dtype=np.float32))
    args = [f(a) for a in (rel_bias, even_norm, even_w_in, even_pool_w, even_pool_scale, even_q_gain, even_k_gain,
                           even_w_out, odd_norm, odd_w_in, odd_q_gain, odd_k_gain, odd_sinks, odd_w_out)]
    x = f(x)
    shared = shared_inputs(*args)
    cores = list(range(8))
    nc = build_fused()
    in_maps = []
    for c in cores:
        d = dict(shared)
        d.update(core_inputs(c, x))
        in_maps.append(d)
    res = run_bass_kernel_spmd(nc, in_maps, core_ids=cores)
    y = np.zeros((64, 128, D), np.float32)
    for c in cores:
        oi = 0
        for (e, m, i) in entries():
            if i < 0:
                continue
            y[block_of(c, m, i)] = res.results[c]["y"][oi]
            oi += 1
    return y.reshape(1, 8192, D)
```

```python
import numpy as np
import ml_dtypes
from contextlib import ExitStack
import concourse.bass as bass
import concourse.mybir as mybir
from concourse.bass_utils import run_bass_kernel_spmd

F32 = mybir.dt.float32
BF16 = mybir.dt.bfloat16
U32 = mybir.dt.uint32
AF = mybir.ActivationFunctionType
ALU = mybir.AluOpType
AX = mybir.AxisListType
NEG = -1.0e30
D = 2048
EPS = 1e-6
QUEUES = ("sync", "scalar", "vector", "gpsimd", "tensor")


class Prog:
    def __init__(self, nc):
        self.nc = nc
        self.ops = []
        self.res = {}
        self.bar = None

    def op(self, eng, fn, reads=(), writes=(), dma=False, waw=False):
        i = len(self.ops)
        deps = set()
        for r in reads:
            st = self.res.setdefault(r, [[], []])
            deps.update(st[0])
            st[1].append(i)
        for w in writes:
            st = self.res.setdefault(w, [[], []])
            if st[1]:
                deps.update(st[1])
                deps.update(st[0])
                st[0] = [i]
                st[1] = []
            elif waw:
                deps.update(st[0])
                st[0] = [i]
            else:
                deps.update(j for j in st[0] if self.ops[j]["eng"] != eng or self.ops[j]["dma"] != dma)
                st[0].append(i)
        if self.bar is not None:
            deps.add(self.bar)
        deps.discard(i)
        self.ops.append(dict(eng=eng, fn=fn, deps=deps, dma=dma, sig=False,
                             key=(writes[0] if (dma and writes) else None)))
        return i

    def barrier(self):
        self.bar = None
        keys = list(self.res.keys())
        i = self.op("vector", lambda e: e.nop(), reads=(), writes=keys, waw=True)
        self.bar = i

    def emit(self, stack):
        nc = self.nc
        ops = self.ops
        for o in ops:
            keep = set()
            for j in o["deps"]:
                d = ops[j]
                if (not d["dma"]) and (not o["dma"]) and d["eng"] == "tensor" and o["eng"] == "tensor":
                    continue
                keep.add(j)
                d["sig"] = True
            o["deps"] = keep
        cnt = {e: 0 for e in QUEUES}
        dcnt = {}
        for o in ops:
            if not o["sig"]:
                continue
            if o["dma"]:
                k = o["key"]
                dcnt[k] = dcnt.get(k, 0) + 16
                o["sem"] = ("dma", k)
                o["val"] = dcnt[k]
            else:
                cnt[o["eng"]] += 1
                o["sem"] = ("eng", o["eng"])
                o["val"] = cnt[o["eng"]]
        sems = {}
        for o in ops:
            if o["sig"] and o["sem"] not in sems:
                sems[o["sem"]] = stack.enter_context(nc.semaphore("s%d" % len(sems)))
        self.nsem = len(sems)
        block = stack.enter_context(nc.Block())
        per = {e: [o for o in ops if o["eng"] == e] for e in QUEUES}

        def run(e, lst):
            waited = {}
            for o in lst:
                need = {}
                for j in o["deps"]:
                    d = ops[j]
                    need[d["sem"]] = max(need.get(d["sem"], 0), d["val"])
                for s, v in need.items():
                    if waited.get(s, 0) >= v:
                        continue
                    e.wait_ge(sems[s], v)
                    waited[s] = v
                ins = o["fn"](e)
                if o["sig"]:
                    ins.then_inc(sems[o["sem"]], 16 if o["dma"] else 1)

        if per["sync"]:
            @block.sync
            def _(e):
                run(e, per["sync"])
        if per["scalar"]:
            @block.scalar
            def _(e):
                run(e, per["scalar"])
        if per["vector"]:
            @block.vector
            def _(e):
                run(e, per["vector"])
        if per["gpsimd"]:
            @block.gpsimd
            def _(e):
                run(e, per["gpsimd"])
        if per["tensor"]:
            @block.tensor
            def _(e):
                run(e, per["tensor"])


class KB:
    def __init__(self):
        self.nc = bass.Bass("TRN2", target_bir_lowering=False)
        self.st = ExitStack()
        self.P = Prog(self.nc)
        self.outkeys = []
        self.banks = None
        self.gi = 0

    def sb(self, name, shape, dt):
        return self.st.enter_context(self.nc.sbuf_tensor(name, shape, dt))

    def arena_init(self, nbytes):
        self.arena = self.sb("arena", [128, nbytes // 2], BF16)
        self.aoff = 0
        self.asize = nbytes

    def ar(self, name, shape, dt):
        esz = 4 if dt in (F32, U32) else 2
        n = 1
        for s in shape[1:]:
            n *= s
        nb = (n * esz + 31) // 32 * 32
        assert self.aoff + nb <= self.asize, ("arena overflow", name, self.aoff, nb, self.asize)
        v = self.arena[0:shape[0], self.aoff // 2:(self.aoff + n * esz) // 2]
        self.aoff += nb
        if esz == 4:
            v = v.bitcast(dt)
        if len(shape) == 3:
            v = v.rearrange("p (a b) -> p a b", a=shape[1])
        elif len(shape) == 4:
            v = v.rearrange("p (a b c) -> p a b c", a=shape[1], b=shape[2])
        return v

    def arena_reset(self):
        self.P.barrier()
        self.aoff = 0

    def din(self, name, shape, dt):
        return self.nc.dram_tensor(name, list(shape), dt, kind="ExternalInput").ap()

    def dout(self, name, shape, dt):
        return self.nc.dram_tensor(name, list(shape), dt, kind="ExternalOutput").ap()

    def dscr(self, name, shape, dt):
        return self.nc.dram_tensor(name, list(shape), dt, kind="Internal").ap()

    def alloc_banks(self):
        self.banks = [self.st.enter_context(self.nc.psum_tensor("bank%d" % i, [128, 512], F32)) for i in range(8)]

    def G(self):
        gl = getattr(self, "glist", None)
        if gl:
            i = self.gi % len(gl)
            self.gi += 1
            return gl[i]
        i = self.gi % 2
        self.gi += 1
        return self.banks[i], ("bank", i)

    def dma(self, q, out, in_, r, w):
        self.P.op(q, lambda e: e.dma_start(out=out, in_=in_), reads=r, writes=w, dma=True)

    def dma_out(self, q, out, in_, r, key):
        self.P.op(q, lambda e: e.dma_start(out=out, in_=in_), reads=r, writes=[key], dma=True)
        if key not in self.outkeys:
            self.outkeys.append(key)

    def act(self, out, in_, func, r, w, **kw):
        self.P.op("scalar", lambda e: e.activation(out=out, in_=in_, func=func, **kw), reads=r, writes=w)

    def vec(self, name, r, w, *a, **kw):
        self.P.op("vector", lambda e: getattr(e, name)(*a, **kw), reads=r, writes=w)

    def pool(self, name, r, w, *a, **kw):
        self.P.op("gpsimd", lambda e: getattr(e, name)(*a, **kw), reads=r, writes=w)

    def mm(self, out, lhsT, rhs, start, stop, r, w):
        self.P.op("tensor", lambda e: e.matmul(out, lhsT=lhsT, rhs=rhs, start=start, stop=stop), reads=r, writes=w)

    def tr(self, out, in_, ident, r, w):
        self.P.op("tensor", lambda e: e.transpose(out=out, in_=in_, identity=ident), reads=r, writes=w)

    def finish(self):
        self.P.op("sync", lambda e: e.nop(), reads=list(self.outkeys))
        self.P.emit(self.st)
        self.st.close()
        return self.nc

    def rmsnorm_T(self, xt, kx, gB, xnT, kxnT, tmp):
        junk, ss, rstd, xn, idt = tmp["junk"], tmp["ss"], tmp["rstd"], tmp["xn"], tmp["idt"]
        kj = tmp.get("kjunk", "junk")
        kxn = tmp.get("kxn", "xn")
        ks = tmp.get("kss", "ss")
        kr = tmp.get("krstd", "rstd")
        self.act(junk, xt, AF.Square, [kx], [kj, ks], accum_out=ss[:])
        self.act(rstd[:], ss[:], AF.Sqrt, [ks], [kr], scale=1.0 / D, bias=EPS)
        self.vec("reciprocal", [kr], [kr], out=rstd[:], in_=rstd[:])
        self.vec("scalar_tensor_tensor", [kx, kr, "gB"], [kxn], out=xn[:], in0=xt, scalar=rstd[:, 0:1],
                 in1=gB[:], op0=ALU.mult, op1=ALU.mult)
        self.transposeN(xn, kxn, xnT, kxnT, 16, 128, idt)

    def transposeN(self, src, ksrc, dst, kdst, n, width, idt, rows=128):
        per = max(1, 1024 // rows)
        per = min(per, 8)
        i = 0
        while i < n:
            m = min(per, n - i)
            bank, kb = self.G()
            pv = bank[:].bitcast(BF16)[0:width, 0:m * rows].rearrange("p (a b) -> p a b", a=m)
            for a in range(m):
                self.tr(pv[:, a, :], src[0:rows, (i + a) * width:(i + a + 1) * width], idt[0:rows, 0:rows],
                        [ksrc, "idt"], [kb])
            self.vec("tensor_copy", [kb], [kdst], out=dst[0:width, i:i + m, 0:rows], in_=pv)
            i += m

RNG = 32.0
NIT = 30
MASKV = -30000.0
CW = 256
RUN = 4
NRUN = 8 // RUN
NE = NRUN * (RUN + 1)
NSLOT = 64
NDUM = 7 * RUN
EVEN_CHUNKS = [("pin", 0, 4), ("pgate", 1024, 4), ("dq", 2048, 4), ("dgate", 3200, 4), ("iq", 4224, 4)]


def entries():
    out = []
    for m in range(NRUN):
        for i in range(-1, RUN):
            out.append((m * (RUN + 1) + i + 1, m, i))
    return out


def nkb_of(m, i):
    return 8 * RUN * m + 7 * RUN + i + 1


def block_of(c, m, i):
    return RUN * (8 * m + c) + i


def build_fused():
    K = KB()
    nc = K.nc
    xs = K.din("xs", [NSLOT, 128, D], F32)
    xq_own = K.din("xq_own", [NE, 128, D], F32)
    xq_prev = K.din("xq_prev", [NE, 128, D], F32)
    g0 = K.din("g0", [1, D], F32)
    w_in0 = K.din("w_in0", [D, 5328], F32)
    wkv0 = K.din("wkv0", [D, 192], F32)
    pool_w = K.din("pool_w", [4, 256, 256], F32)
    pscale = K.din("pscale", [1, 1024], F32)
    qgain0 = K.din("qgain0", [1, 64], F32)
    kgain0 = K.din("kgain0", [1, 64], F32)
    w_out0 = K.din("w_out0", [D, D], F32)
    U = K.din("U", [128, 16, 1152], F32)
    rb31 = K.din("rb31", [1, 16], F32)
    ident = K.din("ident", [128, 128], BF16)
    i4 = K.din("i4", [128, 512], BF16)
    diagm = K.din("diagm", [128, 128], F32)
    vadd0 = K.din("vadd0", [1, NDUM * 128], BF16)
    vadd1 = K.din("vadd1", [1, NDUM * 128], BF16)
    ktin = K.din("kt", [128, NE], F32)
    poolA = K.din("poolA", [NE, 2, 128, 512], BF16)
    invcnt = K.din("invcnt", [NE, 1, 512], F32)
    g1 = K.din("g1", [1, D], F32)
    w_in1 = K.din("w_in1", [D, 4608], F32)
    qgain1 = K.din("qgain1", [1, 64], F32)
    kgain1 = K.din("kgain1", [1, 64], F32)
    sinks = K.din("sinks", [1, 32], F32)
    w_out1 = K.din("w_out1", [D, D], F32)
    mbt_in = K.din("mbt", [128, 2, 32, 128], F32)
    flagp = K.din("flagp", [128, NE], F32)
    y = K.dout("y", [8, 128, D], F32)
    W0C = [c0 + ci * CW for (_, c0, n) in EVEN_CHUNKS for ci in range(n)] + [5312]
    W1C = list(range(0, 4608, CW))
    WOC = list(range(0, D, CW))
    wbi0 = K.dscr("wbi0", [len(W0C), 128, 16, CW], BF16)
    wbo0 = K.dscr("wbo0", [len(WOC), 128, 16, CW], BF16)
    wbi1 = K.dscr("wbi1", [len(W1C), 128, 16, CW], BF16)
    wbo1 = K.dscr("wbo1", [len(WOC), 128, 16, CW], BF16)
    bts = K.dscr("bts", [8, 2, 128, 2, 1024], BF16)
    x1s = K.dscr("x1s", [NE, 128, D], F32)

    G = [K.st.enter_context(nc.psum_tensor("G%d" % i, [128, 512], F32)) for i in range(2)]
    K.banks = G
    Lb = [K.st.enter_context(nc.psum_tensor("L%d" % i, [128, 1024], F32)) for i in range(2)]
    Ob = K.st.enter_context(nc.psum_tensor("O", [128, 8, 128], F32))
    idt = K.sb("idt", [128, 128], BF16)
    gB = K.sb("gB", [128, D], F32)
    KIK = K.sb("KIK", [128, NSLOT * 128], BF16)
    Vaug = K.sb("Vaug", [128, NSLOT, 65], BF16)
    K.arena_init(175 * 1024)
    sm = {n: K.sb(n, [128, 1], F32) for n in ["ss", "rstd", "lo", "thr", "cnt"]}

    K.dma("sync", idt[:], ident[:, :], [], ["idt"])
    K.dma("sync", gB[:], g0[0:1, :].to_broadcast([128, D]), [], ["gB"])
    def cast_weights(src, dst, cols, ncol, key):
        for t, c0 in enumerate(cols):
            w_ = min(CW, ncol - c0)
            K.dma("gpsimd", dst[t][:, :, 0:w_], src[:, c0:c0 + w_].rearrange("(k p) n -> p k n", p=128), [], [key])

    cast_weights(w_in0, wbi0, W0C, 5328, "wbi0")

    K.glist = [(G[0], ("bank", 0)), (G[1], ("bank", 1)), (Lb[0][:, 0:512], ("La", 0)), (Lb[0][:, 512:1024], ("La", 1)),
               (Lb[1][:, 0:512], ("La", 2)), (Lb[1][:, 512:1024], ("La", 3))]
    NBA = 4
    kgB = K.ar("kgB", [128, 64], F32)
    wt192 = K.ar("wt192", [128, 16, 192], BF16)
    xts = [K.ar("xts%d" % i, [128, D], F32) for i in range(NBA)]
    junkA = [K.ar("junkA%d" % i, [128, D], BF16) for i in range(NBA)]
    xnA = [K.ar("xnA%d" % i, [128, D], BF16) for i in range(NBA)]
    xnTA = [K.ar("xnTA%d" % i, [128, 16, 128], BF16) for i in range(NBA)]
    ssA = [K.ar("ssA%d" % i, [128, 1], F32) for i in range(NBA)]
    rsA = [K.ar("rsA%d" % i, [128, 1], F32) for i in range(NBA)]
    kssA = [K.ar("kssA%d" % i, [128, 1], F32) for i in range(NBA)]
    krsA = [K.ar("krsA%d" % i, [128, 1], F32) for i in range(NBA)]
    kjunkA = [K.ar("kjunkA%d" % i, [128, 64], F32) for i in range(NBA)]
    kvb = [K.ar("kvb%d" % i, [128, 128], BF16) for i in range(NBA)]
    K.dma("sync", kgB[:], kgain0[0:1, :].to_broadcast([128, 64]), [], ["kgB"])
    K.dma("gpsimd", wt192[:], wkv0.rearrange("(k p) n -> p k n", p=128), [], ["wt192"])
    K.vec("memset", [], ["Vaug"], Vaug[:, :, 64:65], 1.0)
    def stage1(u):
        i = u % NBA
        xt_ = xts[i]
        tmpA = dict(junk=junkA[i], kjunk=("junkA", i), ss=ssA[i], kss=("ssA", i), rstd=rsA[i], krstd=("rsA", i),
                    xn=xnA[i], kxn=("xnA", i), idt=idt)
        K.dma("sync", xt_[:], xs[u], [], [("xts", i)])
        K.rmsnorm_T(xt_[:], ("xts", i), gB, xnTA[i], ("xnTA", i), tmpA)

    def stage2(u):
        i = u % NBA
        bank, kbk = K.G()
        for k in range(16):
            K.mm(bank[:, 0:192], xnTA[i][:, k, :], wt192[:, k, :], k == 0, k == 15, [("xnTA", i), "wt192"], [kbk])
        kss, krs, kjunk = kssA[i], krsA[i], kjunkA[i]
        K.act(kjunk[:], bank[:, 64:128], AF.Square, [kbk], [("kjunkA", i), ("kssA", i)], accum_out=kss[:])
        K.act(krs[:], kss[:], AF.Sqrt, [("kssA", i)], [("krsA", i)], scale=1.0 / 64, bias=EPS)
        K.vec("reciprocal", [("krsA", i)], [("krsA", i)], out=krs[:], in_=krs[:])
        K.vec("scalar_tensor_tensor", [kbk, ("krsA", i), "kgB"], [("kvb", i)], out=kvb[i][:, 64:128], in0=bank[:, 64:128],
              scalar=krs[:, 0:1], in1=kgB[:], op0=ALU.mult, op1=ALU.mult)
        K.vec("tensor_copy", [kbk], [("kvb", i)], out=kvb[i][:, 0:64], in_=bank[:, 0:64])
        K.vec("tensor_copy", [kbk], ["Vaug"], out=Vaug[:, u, 0:64], in_=bank[:, 128:192])
        bank2, kb2 = K.G()
        pv = bank2[:].bitcast(BF16)[:, 0:128]
        K.tr(pv, kvb[i][:, :], idt[:], [("kvb", i), "idt"], [kb2])
        K.vec("tensor_copy", [kb2], ["KIK"], out=KIK[:, u * 128:(u + 1) * 128], in_=pv)

    stage1(0)
    stage1(1)
    for u in range(NSLOT):
        stage2(u)
        if u + 2 < NSLOT:
            stage1(u + 2)
    K.arena_reset()
    K.glist = None
    cast_weights(w_out0, wbo0, WOC, D, "wbo0")
    cast_weights(w_in1, wbi1, W1C, 4608, "wbi1")
    cast_weights(w_out1, wbo1, WOC, D, "wbo1")

    I4 = K.ar("I4", [128, 512], BF16)
    score = K.ar("score", [128, 8192], F32)
    mb = K.ar("mb", [128, 8192], BF16)
    junk = K.ar("junk", [128, D], BF16)
    xtb = [K.ar("xt%d" % i, [128, D], F32) for i in range(2)]
    xnb = [K.ar("xn%d" % i, [128, D], BF16) for i in range(2)]
    xnT = K.ar("xnT", [128, 16, 128], BF16)
    yT = K.ar("yT", [128, 16, 128], BF16)
    xnTp = yT
    wts = [K.ar("wt%d" % i, [128, 16, CW], BF16) for i in range(2)]
    pa_own = K.ar("pa_own", [128, 1024], BF16)
    pa_prev = K.ar("pa_prev", [128, 1024], BF16)
    sgp = K.ar("sgp", [128, 1024], F32)
    dqf = K.ar("dqf", [128, 1024], F32)
    sgdb = [K.ar("sgd%d" % i, [128, 1024], F32) for i in range(2)]
    iwc = K.ar("iwc", [128, 16], F32)
    C = K.ar("C", [128, 16, 128], BF16)
    QIQb = [K.ar("QIQ%d" % i, [128, 16, 128], BF16) for i in range(2)]
    rb = [K.ar("r%d" % i, [128, 512], F32) for i in range(4)]
    PT = [K.ar("PT%d" % i, [128, 1024], BF16) for i in range(2)]
    bhl = [K.ar("bhl%d" % i, [128, 2, 1024], BF16) for i in range(2)]
    rb31B = K.ar("rb31B", [128, 16], F32)
    pooledT = K.ar("pooledT", [128, 8, 128], BF16)
    pA = K.ar("pA", [128, 2, 512], BF16)
    invc = K.ar("invc", [128, 512], F32)
    pw = K.ar("pw", [128, 4, 2, 256], BF16)
    qg8 = K.ar("qg8", [128, 64], F32)
    vaddB = K.ar("vaddB", [128, NDUM * 128], BF16)
    dm = K.ar("dm", [128, 128], F32)
    ktc = K.ar("ktc", [128, NE], F32)
    qss = K.ar("qss", [128, 16], F32)
    qrs = K.ar("qrs", [128, 16], F32)
    rden = K.ar("rden", [128, 8], F32)
    of = K.ar("of", [128, 8, 64], F32)
    ssB = K.ar("ssB", [128, 1], F32)
    rstdB = K.ar("rstdB", [128, 1], F32)
    SK = [("score", c) for c in range(16)]
    bstage = sgp.rearrange("p (a b) -> p a b", a=8)
    bstage2 = dqf.rearrange("p (a b) -> p a b", a=8)

    K.dma("sync", I4[:], i4[:, :], [], ["I4"])
    K.dma("sync", rb31B[:], rb31[0:1, :].to_broadcast([128, 16]), [], ["rb31B"])
    K.dma("sync", qg8[:], qgain0[0:1, :].to_broadcast([128, 64]), [], ["qg8"])
    K.vec("tensor_scalar_mul", ["qg8"], ["qg8"], out=qg8[:], in0=qg8[:], scalar1=0.125)
    K.dma("sync", vaddB[:], vadd0[0:1, :].to_broadcast([128, NDUM * 128]), [], ["vaddB"])
    K.dma("sync", dm[:], diagm[:, :], [], ["dm"])
    K.dma("sync", ktc[:], ktin[:, :], [], ["ktc"])
    pwf = score[:, 0:2048].rearrange("p (g k d) -> p g k d", g=4, k=2)
    K.dma("sync", pwf, pool_w.rearrange("g (k p) d -> p g k d", p=128), [], [SK[0]])
    K.dma("sync", score[:, 2048:3072], pscale[0:1, :].to_broadcast([128, 1024]), [], [SK[1]])
    for gg in range(4):
        K.vec("tensor_tensor", [SK[0], SK[1]], ["pw"], out=pw[:, gg, :, :], in0=pwf[:, gg, :, :],
              in1=score[:, 2048 + gg * 256:2048 + (gg + 1) * 256].unsqueeze(1).to_broadcast([128, 2, 256]), op=ALU.mult)
    for dl in range(8):
        for hh in range(2):
            hs = slice(hh * 8, hh * 8 + 8)
            K.dma("sync", bstage, U[:, hs, dl * 128:(dl + 1) * 128], [], ["sgp"])
            K.vec("tensor_tensor", ["sgp", "rb31B"], ["sgp"], out=bstage, in0=bstage,
                  in1=rb31B[:, hs].unsqueeze(2).to_broadcast([128, 8, 128]), op=ALU.subtract)
            i = (dl * 2 + hh) % 2
            bv = bhl[i]
            kb_ = ("bhl", i)
            K.vec("tensor_copy", ["sgp"], [kb_], out=bv[:, 0, :], in_=sgp[:])
            K.vec("tensor_tensor", ["sgp", kb_], ["dqf"], out=dqf[:], in0=sgp[:], in1=bv[:, 0, :], op=ALU.subtract)
            K.vec("tensor_copy", ["dqf"], [kb_], out=bv[:, 1, :], in_=dqf[:])
            K.dma("sync", bts[dl, hh], bv[:], [kb_], [("bts", i)])

    wslot = [0]

    wtl = [wts]

    def load_w(src, cols, c0, rkey):
        nb = len(wtl[0])
        i = wslot[0] % nb
        wslot[0] += 1
        K.dma("sync", wtl[0][i][:], src[cols.index(c0)], [rkey], [("wt", i)])
        return wtl[0][i], ("wt", i)

    ents = entries()

    def geom(e):
        _, m, ii = ents[e]
        nkb = nkb_of(m, ii)
        N = 128 * nkb
        chunks = [(c0, min(512, N - c0)) for c0 in range(0, N, 512)]
        return nkb, N, chunks

    def front(e):
        p = e % 2
        nkb, N, chunks = geom(e)
        nch = len(chunks)
        xt, xn, sgd, QIQ = xtb[p], xnb[p], sgdb[p], QIQb[p]
        kxt, kxn, ksgd, kQ = ("xt", p), ("xn", p), ("sgd", p), ("QIQ", p)
        tmp = dict(junk=junk, kjunk="junk", ss=ssB, rstd=rstdB, xn=xn, kxn=kxn, idt=idt)
        if e == 1:
            K.dma("sync", vaddB[:], vadd1[0:1, :].to_broadcast([128, NDUM * 128]), [], ["vaddB"])
        K.dma("gpsimd", xt[:], xq_prev[e], [], [kxt])
        K.rmsnorm_T(xt[:], kxt, gB, xnTp, "yT", tmp)
        K.dma("gpsimd", xt[:], xq_own[e], [], [kxt])
        K.rmsnorm_T(xt[:], kxt, gB, xnT, "xnT", tmp)
        K.dma("gpsimd", pA[:], poolA[e].rearrange("a p n -> p a n"), [], ["pA"])
        K.dma("gpsimd", invc[:], invcnt[e, 0:1, :].to_broadcast([128, 512]), [], ["invc"])

        def proj_chunk(kind, col0, ci):
            c0 = col0 + ci * CW
            wt, kw = load_w(wbi0, W0C, c0, "wbi0")
            bank, kbk = K.G()
            for k in range(16):
                K.mm(bank[:, 0:CW], xnT[:, k, :], wt[:, k, :], k == 0, k == 15, ["xnT", kw], [kbk])
            lo_, hi_ = ci * CW, (ci + 1) * CW
            if kind == "pin":
                K.vec("tensor_copy", [kbk], ["pa_own"], out=pa_own[:, lo_:hi_], in_=bank[:, 0:CW])
                bank2, kb2 = K.G()
                for k in range(16):
                    K.mm(bank2[:, 0:CW], xnTp[:, k, :], wt[:, k, :], k == 0, k == 15, ["yT", kw], [kb2])
                K.vec("tensor_copy", [kb2], ["pa_prev"], out=pa_prev[:, lo_:hi_], in_=bank2[:, 0:CW])
            elif kind == "pgate":
                K.act(sgp[:, lo_:hi_], bank[:, 0:CW], AF.Silu, [kbk], ["sgp"])
            elif kind == "dq":
                K.vec("tensor_copy", [kbk], ["dqf"], out=dqf[:, lo_:hi_], in_=bank[:, 0:CW])
            elif kind == "dgate":
                K.act(sgd[:, lo_:hi_], bank[:, 0:CW], AF.Silu, [kbk], [ksgd])
            elif kind == "iq":
                K.vec("tensor_copy", [kbk], ["C"], out=C[:, ci * 4:(ci + 1) * 4, 0:64],
                      in_=bank[:, 0:CW].rearrange("p (h d) -> p h d", d=64))

        cmap = {kind: col0 for (kind, col0, n) in EVEN_CHUNKS}
        for kind in ("iq", "dq"):
            for ci in range(4):
                proj_chunk(kind, cmap[kind], ci)
        i = wslot[0] % 2
        wslot[0] += 1
        K.dma("sync", wts[i][:, :, 0:16], wbi0[W0C.index(5312)][:, :, 0:16], ["wbi0"], [("wt", i)])
        bank, kbk = K.G()
        for k in range(16):
            K.mm(bank[:, 0:16], xnT[:, k, :], wts[i][:, k, 0:16], k == 0, k == 15, ["xnT", ("wt", i)], [kbk])
        K.vec("tensor_scalar_mul", [kbk], ["iwc"], out=iwc[:], in0=bank[:, 0:16], scalar1=1.0 / 32.0)
        K.act(score[:, 0:1024], dqf[:], AF.Square, ["dqf"], [SK[0], SK[1]])
        K.vec("tensor_reduce", [SK[0], SK[1]], ["qss"], out=qss[:], in_=score[:, 0:1024].rearrange("p (h d) -> p h d", d=64),
              axis=AX.X, op=ALU.add)
        K.act(qrs[:], qss[:], AF.Sqrt, ["qss"], ["qrs"], scale=1.0 / 64, bias=EPS)
        K.vec("reciprocal", ["qrs"], ["qrs"], out=qrs[:], in_=qrs[:])
        dq3 = dqf[:].rearrange("p (h d) -> p h d", d=64)
        K.vec("tensor_tensor", ["dqf", "qrs"], ["dqf"], out=dq3, in0=dq3,
              in1=qrs[:].unsqueeze(2).to_broadcast([128, 16, 64]), op=ALU.mult)
        K.vec("tensor_tensor", ["dqf", "qg8"], ["C"], out=C[:, :, 64:128], in0=dq3,
              in1=qg8[:].unsqueeze(1).to_broadcast([128, 16, 64]), op=ALU.mult)
        K.transposeN(C[:].rearrange("p h d -> p (h d)"), "C", QIQ, kQ, 16, 128, idt)
        tasks = [(kind, cmap[kind], ci) for kind in ("pin", "pgate", "dgate") for ci in range(4)]
        ibanks = [(G[0][:, :], ("bank", 0)), (G[1][:, :], ("bank", 1)), (Lb[0][:, 0:512], ("L", 0)), (Lb[1][:, 0:512], ("L", 1))]
        nunits = nch * 16
        stride = max(1, nunits // (len(tasks) + 1))
        ri = 0
        for ch, (c0, cw) in enumerate(chunks):
            for h in range(16):
                bank, kbk = ibanks[ri % 4]
                K.mm(bank[:, 0:cw], QIQ[0:64, h, :], KIK[0:64, c0:c0 + cw], True, True, [kQ, "KIK"], [kbk])
                r = rb[ri % len(rb)]
                kr = ("r", ri % len(rb))
                ri += 1
                K.act(r[:, 0:cw], bank[:, 0:cw], AF.Relu, [kbk], [kr])
                sc = score[:, c0:c0 + cw]
                if h == 0:
                    K.vec("tensor_scalar_mul", [kr, "iwc"], [SK[ch]], out=sc, in0=r[:, 0:cw], scalar1=iwc[:, 0:1])
                else:
                    K.vec("scalar_tensor_tensor", [kr, "iwc", SK[ch]], [SK[ch]], out=sc, in0=r[:, 0:cw], scalar=iwc[:, h:h + 1],
                          in1=sc, op0=ALU.mult, op1=ALU.add)
                if tasks and ri % stride == 0:
                    proj_chunk(*tasks.pop(0))
        while tasks:
            proj_chunk(*tasks.pop(0))
        nv = min(N, NDUM * 128)
        vk = SK[:(nv + 511) // 512]
        K.vec("tensor_tensor", vk + ["vaddB"], vk, out=score[:, 0:nv], in0=score[:, 0:nv], in1=vaddB[:, 0:nv], op=ALU.add)
        K.vec("tensor_tensor", [SK[nch - 1], "dm"], [SK[nch - 1]], out=score[:, N - 128:N], in0=score[:, N - 128:N],
              in1=dm[:], op=ALU.add)
        for half in range(2):
            bank, kbk = K.G()
            for gi in range(2):
                gg = half * 2 + gi
                for kc in range(2):
                    cs = slice(gg * 256 + kc * 128, gg * 256 + (kc + 1) * 128)
                    o_ = bank[:, (gi * 2 + kc) * 128:(gi * 2 + kc + 1) * 128]
                    K.mm(o_, pa_prev[:, cs], pA[:, 0, gg * 128:(gg + 1) * 128], True, False, ["pa_prev", "pA"], [kbk])
                    K.mm(o_, pa_own[:, cs], pA[:, 1, gg * 128:(gg + 1) * 128], False, True, ["pa_own", "pA"], [kbk])
            K.vec("tensor_tensor", [kbk, "invc"], ["pooledT"],
                  out=pooledT[:, half * 4:(half + 1) * 4, :].rearrange("p (g k) t -> p g k t", g=2),
                  in0=bank[:].rearrange("p (g k t) -> p g k t", g=2, k=2),
                  in1=invc[:, half * 256:(half + 1) * 256].rearrange("p (g t) -> p g t", g=2).unsqueeze(2).to_broadcast([128, 2, 2, 128]),
                  op=ALU.mult)
        for half in range(2):
            bank, kbk = K.G()
            for gi in range(2):
                gg = half * 2 + gi
                for kc in range(2):
                    K.mm(bank[:, gi * 256:(gi + 1) * 256], pooledT[:, gg * 2 + kc, :], pw[:, gg, kc, :], kc == 0, kc == 1,
                         ["pooledT", "pw"], [kbk])
            K.vec("tensor_tensor", [kbk, "sgp"], [kxn], out=xn[:, half * 512:(half + 1) * 512], in0=bank[:],
                  in1=sgp[:, half * 512:(half + 1) * 512], op=ALU.mult)

    lo, thr = sm["lo"], sm["thr"]
    U8 = mybir.dt.uint8
    junk8 = junk.bitcast(U8)
    JW = 4096
    cnt2 = K.ar("cnt2", [128, 2], F32)
    flagf = K.ar("flagf", [128, 1], F32)

    def bisect_iters(e, it0, it1):
        nkb, N, chunks = geom(e)
        SKn = SK[:len(chunks)]
        if it0 == 0:
            K.vec("memset", [], ["lo"], lo[:], -RNG)
            K.vec("memset", [], ["cnt2", ("cnt2", 0), ("cnt2", 1)], cnt2[:], 0.0)
        for it in range(it0, it1):
            step = 2.0 * RNG / (2.0 ** (it + 1))
            K.vec("tensor_scalar_add", ["lo"], ["thr"], out=thr[:], in0=lo[:], scalar1=step)
            for ci, a in enumerate(range(0, N, JW)):
                b = min(N, a + JW)
                K.vec("tensor_scalar", SKn + ["thr"], ["junk", ("cnt2", ci)], out=junk8[:, 0:b - a], in0=score[:, a:b],
                      scalar1=thr[:, 0:1], scalar2=None, op0=ALU.is_ge, op1=ALU.add, accum_out=cnt2[:, ci:ci + 1])
            K.vec("scalar_tensor_tensor", [("cnt2", 0), ("cnt2", 1), "cnt2", "ktc"], ["flagf"], out=flagf[:], in0=cnt2[:, 0:1],
                  scalar=cnt2[:, 1:2], in1=ktc[:, e:e + 1], op0=ALU.add, op1=ALU.is_ge)
            K.vec("scalar_tensor_tensor", ["flagf", "lo"], ["lo"], out=lo[:], in0=flagf[:], scalar=step, in1=lo[:],
                  op0=ALU.mult, op1=ALU.add)

    def mbwrite(e):
        nkb, N, chunks = geom(e)
        SKn = SK[:len(chunks)]
        K.vec("tensor_scalar", SKn + ["lo"], ["mb"], out=mb[:, 0:N], in0=score[:, 0:N], scalar1=lo[:, 0:1],
              scalar2=MASKV, op0=ALU.is_lt, op1=ALU.mult)

    lbi = [0, 0]

    def back_attn(e, hh):
        p = e % 2
        nkb, N, chunks = geom(e)
        xn, sgd, QIQ = xnb[p], sgdb[p], QIQb[p]
        kxn, ksgd, kQ = ("xn", p), ("sgd", p), ("QIQ", p)
        for kb in range(nkb):
            dl = nkb - 1 - kb
            li = lbi[0]
            lbi[0] += 1
            L = Lb[li % 2]
            kL = ("L", li % 2)
            pt = PT[li % 2]
            kpt = ("PT", li % 2)
            near = dl < 8
            if near:
                bi = lbi[1]
                lbi[1] += 1
                bv = bhl[bi % 2]
                kbv = ("bhl", bi % 2)
                K.dma("gpsimd", bv[:], bts[dl, hh], [("bts", 0), ("bts", 1)], [kbv])
            for n in range(2):
                Ls = L[:, n * 512:(n + 1) * 512]
                h0 = hh * 8 + n * 4
                K.mm(Ls, KIK[64:128, kb * 128:(kb + 1) * 128], QIQ[64:128, h0:h0 + 4, :], True, False,
                     ["KIK", kQ], [kL])
                K.mm(Ls, mb[:, kb * 128:(kb + 1) * 128], I4[:], False, not near, ["mb", "I4"], [kL])
                if near:
                    K.mm(Ls, idt[:], bv[:, 0, n * 512:(n + 1) * 512], False, False, ["idt", kbv], [kL])
                    K.mm(Ls, idt[:], bv[:, 1, n * 512:(n + 1) * 512], False, True, ["idt", kbv], [kL])
            K.act(pt[:], L[:], AF.Exp, [kL], [kpt])
            for i8 in range(8):
                K.mm(Ob[:, i8, 0:65], pt[:, i8 * 128:(i8 + 1) * 128], Vaug[:, kb, :], (kb == 0 and i8 % 4 == 0),
                     kb == nkb - 1, [kpt, "Vaug"], ["O"])
        K.vec("reciprocal", ["O"], ["rden"], out=rden[:], in_=Ob[:, :, 64])
        K.vec("tensor_tensor", ["O", "rden"], ["of"], out=of[:], in0=Ob[:, :, 0:64],
              in1=rden[:].unsqueeze(2).to_broadcast([128, 8, 64]), op=ALU.mult)
        K.vec("tensor_tensor", ["of", ksgd], [kxn], out=xn[:, 1024 + hh * 512:1024 + (hh + 1) * 512],
              in0=of[:].rearrange("p h d -> p (h d)"), in1=sgd[:, hh * 512:(hh + 1) * 512], op=ALU.mult)

    def back_out(e):
        p = e % 2
        xt, xn = xtb[p], xnb[p]
        kxt, kxn = ("xt", p), ("xn", p)
        K.transposeN(xn, kxn, yT, "yT", 16, 128, idt)
        for n in range(D // CW):
            wt, kw = load_w(wbo0, WOC, n * CW, "wbo0")
            bank, kbk = K.G()
            for k in range(16):
                K.mm(bank[:, 0:CW], yT[:, k, :], wt[:, k, :], k == 0, k == 15, ["yT", kw], [kbk])
            K.vec("tensor_tensor", [kbk, kxt], [kxt], out=xt[:, n * CW:(n + 1) * CW], in0=bank[:, 0:CW],
                  in1=xt[:, n * CW:(n + 1) * CW], op=ALU.add)
        K.dma("gpsimd", x1s[e], xt[:], [kxt], [("x1s", e)])

    H1 = NIT // 2
    for e in range(NE):
        front(e)
        if e == 0:
            bisect_iters(e, 0, NIT)
        else:
            bisect_iters(e, 0, H1)
            back_attn(e - 1, 0)
            bisect_iters(e, H1, NIT)
            back_attn(e - 1, 1)
            back_out(e - 1)
        mbwrite(e)
    back_attn(NE - 1, 0)
    back_attn(NE - 1, 1)
    back_out(NE - 1)
    K.arena_reset()

    xt = K.ar("xt", [128, D], F32)
    xn = K.ar("xn", [128, D], BF16)
    junk = K.ar("junk", [128, D], BF16)
    xnT = K.ar("xnT", [128, 16, 128], BF16)
    xnTp = K.ar("xnTp", [128, 16, 128], BF16)
    wts = [K.ar("wt%d" % i, [128, 16, CW], BF16) for i in range(4)]
    wtl[0] = wts
    RES1 = [2048, 2304, 0]
    wres = {c0: K.ar("wres%d" % c0, [128, 16, CW], BF16) for c0 in RES1}
    qf = K.ar("qf", [128, D], F32)
    sq = K.ar("sq", [128, D], F32)
    sg = K.ar("sg", [128, D], F32)
    kf = [K.ar("kf%d" % i, [128, 256], F32) for i in range(2)]
    Cq = K.ar("Cq", [128, D], BF16)
    Ck = [K.ar("Ck%d" % i, [128, 256], BF16) for i in range(2)]
    qT = K.ar("qT", [64, 32, 128], BF16)
    kT = [K.ar("kT%d" % i, [64, 4, 128], BF16) for i in range(2)]
    Va = [K.ar("Va%d" % i, [128, 4, 65], BF16) for i in range(2)]
    MBT = K.ar("MBT", [128, 2, 32, 128], F32)
    S = [K.ar("S%d" % i, [128, 1024], F32) for i in range(2)]
    PT = [K.ar("PT%d" % i, [128, 1024], BF16) for i in range(2)]
    qg8 = K.ar("qg8", [128, 64], F32)
    kgB = K.ar("kgB", [128, 64], F32)
    sinkE = K.ar("sinkE", [128, 32], F32)
    flg = K.ar("flg", [128, NE], F32)
    qss = K.ar("qss", [128, 32], F32)
    kss = K.ar("kss", [128, 4], F32)
    den = K.ar("den", [128, 8], F32)
    of = K.ar("of", [128, 8, 64], F32)
    tmp = dict(junk=junk, ss=sm["ss"], rstd=sm["rstd"], xn=xn, idt=idt)

    K.dma("sync", gB[:], g1[0:1, :].to_broadcast([128, D]), [], ["gB"])
    K.dma("sync", qg8[:], qgain1[0:1, :].to_broadcast([128, 64]), [], ["qg8"])
    K.vec("tensor_scalar_mul", ["qg8"], ["qg8"], out=qg8[:], in0=qg8[:], scalar1=0.125)
    K.dma("sync", kgB[:], kgain1[0:1, :].to_broadcast([128, 64]), [], ["kgB"])
    K.dma("sync", sinkE[:], sinks[0:1, :].to_broadcast([128, 32]), [], ["sinkE"])
    K.act(sinkE[:], sinkE[:], AF.Exp, ["sinkE"], ["sinkE"])
    K.dma("sync", flg[:], flagp[:, :], [], ["flg"])
    K.dma("sync", MBT[:], mbt_in[:, :, :, :], [], ["MBT"])
    wslot[0] = 0
    for c0 in RES1:
        K.dma("sync", wres[c0][:], wbi1[W1C.index(c0)], ["wbi1"], [("wres", c0)])
    _load_w = load_w

    def load_w(src, cols, c0, rkey):
        if src is wbi1 and c0 in wres:
            return wres[c0], ("wres", c0)
        return _load_w(src, cols, c0, rkey)

    def proj(xT, kxT, wt, kw):
        bank, kbk = K.G()
        for k in range(16):
            K.mm(bank[:, 0:CW], xT[:, k, :], wt[:, k, :], k == 0, k == 15, [kxT, kw], [kbk])
        return bank, kbk

    def headnorm(src, ksrc, nh, ssb, kss_, gainB, kg, dst, kdst):
        K.act(sq[:, 0:nh * 64], src, AF.Square, [ksrc], ["sq"])
        K.vec("tensor_reduce", ["sq"], [kss_], out=ssb[:], in_=sq[:, 0:nh * 64].rearrange("p (h d) -> p h d", d=64),
              axis=AX.X, op=ALU.add)
        K.act(ssb[:], ssb[:], AF.Sqrt, [kss_], [kss_], scale=1.0 / 64, bias=EPS)
        K.vec("reciprocal", [kss_], [kss_], out=ssb[:], in_=ssb[:])
        s3 = src.rearrange("p (h d) -> p h d", d=64)
        K.vec("tensor_tensor", [ksrc, kss_], [ksrc], out=s3, in0=s3,
              in1=ssb[:].unsqueeze(2).to_broadcast([128, nh, 64]), op=ALU.mult)
        K.vec("tensor_tensor", [ksrc, kg], [kdst], out=dst.rearrange("p (h d) -> p h d", d=64), in0=s3,
              in1=gainB[:].unsqueeze(1).to_broadcast([128, nh, 64]), op=ALU.mult)

    oi = 0
    for (e, m, ii) in entries():
        if ii < 0:
            continue
        K.dma("sync", xt[:], x1s[e - 1], [("x1s", e - 1)], ["xt"])
        K.rmsnorm_T(xt[:], "xt", gB, xnTp, "xnTp", tmp)
        K.dma("sync", xt[:], x1s[e], [("x1s", e)], ["xt"])
        K.rmsnorm_T(xt[:], "xt", gB, xnT, "xnT", tmp)
        for ci in range(8):
            wt, kw = load_w(wbi1, W1C, ci * CW, "wbi1")
            bank, kbk = proj(xnT, "xnT", wt, kw)
            K.vec("tensor_copy", [kbk], ["qf"], out=qf[:, ci * CW:(ci + 1) * CW], in_=bank[:, 0:CW])
        wt, kw = load_w(wbi1, W1C, 2048, "wbi1")
        bank, kbk = proj(xnT, "xnT", wt, kw)
        K.vec("tensor_copy", [kbk], [("kf", 1)], out=kf[1][:], in_=bank[:, 0:CW])
        bank, kbk = proj(xnTp, "xnTp", wt, kw)
        K.vec("tensor_copy", [kbk], [("kf", 0)], out=kf[0][:], in_=bank[:, 0:CW])
        wt, kw = load_w(wbi1, W1C, 2304, "wbi1")
        bank, kbk = proj(xnT, "xnT", wt, kw)
        K.vec("tensor_copy", [kbk], [("Va", 1)], out=Va[1][:, :, 0:64], in_=bank[:, 0:CW].rearrange("p (h d) -> p h d", d=64))
        K.vec("memset", [], [("Va", 1)], Va[1][:, :, 64:65], 1.0)
        bank, kbk = proj(xnTp, "xnTp", wt, kw)
        K.vec("tensor_scalar_mul", [kbk, "flg"], [("Va", 0)], out=Va[0][:, :, 0:64],
              in0=bank[:, 0:CW].rearrange("p (h d) -> p h d", d=64), scalar1=flg[:, e:e + 1])
        K.vec("tensor_copy", ["flg"], [("Va", 0)], out=Va[0][:, :, 64:65],
              in_=flg[:, e:e + 1].unsqueeze(1).to_broadcast([128, 4, 1]))
        for ci in range(8):
            wt, kw = load_w(wbi1, W1C, 2560 + ci * CW, "wbi1")
            bank, kbk = proj(xnT, "xnT", wt, kw)
            K.act(sg[:, ci * CW:(ci + 1) * CW], bank[:, 0:CW], AF.Silu, [kbk], ["sg"])
        headnorm(qf[:], "qf", 32, qss, "qss", qg8, "qg8", Cq[:], "Cq")
        K.transposeN(Cq, "Cq", qT, "qT", 32, 64, idt)
        for i in range(2):
            headnorm(kf[i][:], ("kf", i), 4, kss, "kss", kgB, "kgB", Ck[i][:], ("Ck", i))
            K.transposeN(Ck[i], ("Ck", i), kT[i], ("kT", i), 4, 64, idt)
        li = 0
        for gk in range(4):
            for half in range(2):
                L = Lb[li % 2]
                kL = ("L", li % 2)
                s_ = S[li % 2]
                kS = ("S", li % 2)
                pt = PT[li % 2]
                kpt = ("PT", li % 2)
                li += 1
                for n in range(2):
                    h0 = gk * 8 + n * 4
                    K.mm(L[:, n * 512:(n + 1) * 512], kT[half][:, gk, :], qT[:, h0:h0 + 4, :], True, True,
                         [("kT", half), "qT"], [kL])
                K.vec("tensor_tensor", [kL, "MBT"], [kS], out=s_[:], in0=L[:],
                      in1=MBT[:, half, gk * 8:(gk + 1) * 8, :].rearrange("p h q -> p (h q)"), op=ALU.add)
                K.act(pt[:], s_[:], AF.Exp, [kS], [kpt])
                for i8 in range(8):
                    K.mm(Ob[:, i8, 0:65], pt[:, i8 * 128:(i8 + 1) * 128], Va[half][:, gk, :],
                         (half == 0 and i8 % 4 == 0), half == 1, [kpt, ("Va", half)], ["O"])
            K.vec("tensor_tensor", ["O", "sinkE"], ["den"], out=den[:], in0=Ob[:, :, 64], in1=sinkE[:, gk * 8:(gk + 1) * 8],
                  op=ALU.add)
            K.vec("reciprocal", ["den"], ["den"], out=den[:], in_=den[:])
            K.vec("tensor_tensor", ["O", "den"], ["of"], out=of[:], in0=Ob[:, :, 0:64],
                  in1=den[:].unsqueeze(2).to_broadcast([128, 8, 64]), op=ALU.mult)
            K.vec("tensor_tensor", ["of", "sg"], ["xn"], out=xn[:, gk * 512:(gk + 1) * 512],
                  in0=of[:].rearrange("p h d -> p (h d)"), in1=sg[:, gk * 512:(gk + 1) * 512], op=ALU.mult)
        K.transposeN(xn, "xn", xnT, "xnT", 16, 128, idt)
        for n in range(D // CW):
            wt, kw = load_w(wbo1, WOC, n * CW, "wbo1")
            bank, kbk = proj(xnT, "xnT", wt, kw)
            K.vec("tensor_tensor", [kbk, "xt"], ["xt"], out=xt[:, n * CW:(n + 1) * CW], in0=bank[:, 0:CW],
                  in1=xt[:, n * CW:(n + 1) * CW], op=ALU.add)
        K.dma_out("sync", y[oi], xt[:], ["xt"], "y")
        oi += 1
    return K.finish()


def bucket_np(n):
    import math
    n = np.maximum(n, 0)
    nf = np.maximum(n, 1).astype(np.float32)
    large = 16 + (np.log(nf / np.float32(16)) / np.float32(math.log(64)) * np.float32(16)).astype(np.int32)
    large = np.minimum(large, 31)
    return np.where(n < 16, n, large)


def shared_inputs(rel_bias, even_norm, even_w_in, even_pool_w, even_pool_scale, even_q_gain, even_k_gain, even_w_out,
                  odd_norm, odd_w_in, odd_q_gain, odd_k_gain, odd_sinks, odd_w_out):
    bf = ml_dtypes.bfloat16
    s = np.arange(128)[:, None]
    n = np.arange(1152)[None, :]
    U = np.ascontiguousarray(rel_bias[bucket_np(np.maximum(n - s, 0))][:, :, :16].transpose(0, 2, 1)).astype(np.float32)
    q = np.arange(128)[:, None]
    ss = np.arange(128)[None, :]
    diagm = np.where(ss <= q, 0.0, NEG).astype(np.float32)
    qq = np.arange(128)[None, :]
    mbt = np.zeros((128, 2, 32, 128), np.float32)
    for half in range(2):
        dist = 128 * (1 - half) + qq - s
        ok = (dist >= 0) & (dist < 128)
        vals = rel_bias[bucket_np(np.maximum(dist, 0))]
        vals = np.where(ok[:, :, None], vals, np.float32(MASKV))
        mbt[:, half] = vals.transpose(0, 2, 1)
    w0 = even_w_in[0]
    wkv0 = np.ascontiguousarray(np.concatenate([w0[:, 5248:5312], w0[:, 3072:3136], w0[:, 3136:3200]], axis=1))
    return {
        "g0": even_norm[0:1], "w_in0": w0, "wkv0": wkv0, "pool_w": even_pool_w[0], "pscale": even_pool_scale[0:1],
        "qgain0": even_q_gain[0:1], "kgain0": even_k_gain[0:1], "w_out0": even_w_out[0], "U": U,
        "rb31": np.ascontiguousarray(rel_bias[31:32, :16]),
        "ident": np.eye(128, dtype=np.float32).astype(bf),
        "i4": np.tile(np.eye(128, dtype=np.float32), (1, 4)).astype(bf), "diagm": diagm,
        "g1": odd_norm[0:1], "w_in1": odd_w_in[0], "qgain1": odd_q_gain[0:1], "kgain1": odd_k_gain[0:1],
        "sinks": odd_sinks[0:1], "w_out1": odd_w_out[0], "mbt": mbt,
    }


def core_inputs(c, x):
    bf = ml_dtypes.bfloat16
    xb = x.reshape(64, 128, D)
    zero = np.zeros((128, D), np.float32)
    blk = lambda b: xb[b] if 0 <= b < 64 else zero
    xs = np.stack([blk(u + RUN * c - 7 * RUN) for u in range(NSLOT)])
    ents = entries()
    xq_own = np.stack([blk(block_of(c, m, i)) for (e, m, i) in ents])
    xq_prev = np.stack([blk(block_of(c, m, i) - 1) for (e, m, i) in ents])
    vadd1 = np.zeros((1, NDUM * 128), np.float32)
    for u in range(NDUM):
        if u + RUN * c - 7 * RUN < 0:
            vadd1[0, u * 128:(u + 1) * 128] = NEG
    vadd0 = vadd1.copy() if block_of(c, 0, -1) >= 0 else np.zeros_like(vadd1)
    kt = np.ones((128, NE), np.float32)
    flagp = np.zeros((128, NE), np.float32)
    poolA = np.zeros((NE, 2, 128, 4, 128), np.float32)
    invcnt = np.ones((NE, 1, 4, 128), np.float32)
    for (e, m, i) in ents:
        b = block_of(c, m, i)
        flagp[:, e] = 1.0 if b - 1 >= 0 else 0.0
        bb = max(b, 0)
        tg = bb * 128 + np.arange(128)
        kt[:, e] = np.minimum(256, tg + 1) if b >= 0 else 1.0
        for gi, w in enumerate((2, 4, 8, 16)):
            cnt = np.minimum(w, tg + 1)
            invcnt[e, 0, gi] = 1.0 / cnt
            t = np.arange(128)[None, :]
            s_ = np.arange(128)[:, None]
            own = ((s_ <= t) & (s_ > t - w)).astype(np.float32) - np.where(s_ == t, cnt[None, :], 0)
            prev = ((s_ - 128 > t - w)).astype(np.float32) if b > 0 else np.zeros((128, 128), np.float32)
            poolA[e, 0, :, gi, :] = prev
            poolA[e, 1, :, gi, :] = own
    return {"xs": np.ascontiguousarray(xs), "xq_own": np.ascontiguousarray(xq_own), "xq_prev": np.ascontiguousarray(xq_prev),
            "vadd0": vadd0.astype(bf), "vadd1": vadd1.astype(bf), "kt": kt, "flagp": flagp,
            "poolA": poolA.reshape(NE, 2, 128, 512).astype(bf), "invcnt": invcnt.reshape(NE, 1, 512)}


def kernel(x, rel_bias, even_norm, even_w_in, even_pool_w, even_pool_scale, even_q_gain, even_k_gain, even_w_out,
           odd_norm, odd_w_in, odd_q_gain, odd_k_gain, odd_sinks, odd_w_out):
    f = lambda a: np.ascontiguousarray(np.asarray(a, dtype=np.float32))
    args = [f(a) for a in (rel_bias, even_norm, even_w_in, even_pool_w, even_pool_scale, even_q_gain, even_k_gain,
                           even_w_out, odd_norm, odd_w_in, odd_q_gain, odd_k_gain, odd_sinks, odd_w_out)]
    x = f(x)
    shared = shared_inputs(*args)
    cores = list(range(8))
    nc = build_fused()
    in_maps = []
    for c in cores:
        d = dict(shared)
        d.update(core_inputs(c, x))
        in_maps.append(d)
    res = run_bass_kernel_spmd(nc, in_maps, core_ids=cores)
    y = np.zeros((64, 128, D), np.float32)
    for c in cores:
        oi = 0
        for (e, m, i) in entries():
            if i < 0:
                continue
            y[block_of(c, m, i)] = res.results[c]["y"][oi]
            oi += 1
    return y.reshape(1, 8192, D)
```
